# Optimizing a Trainium2 kernel written in Bass

```python
import jax
import jax.numpy as jnp
from jax import lax
import numpy as np

D_MODEL = 1024
BATCH = 4
SEQ = 8192
DEPTH = 1

PLE_DIM = 256
D_FF = 2816
N_HEADS = 8
N_KV_HEADS = 2
HEAD_DIM = 64
HEADS_PER_GROUP = N_HEADS // N_KV_HEADS
NSA_WIDTH = N_HEADS * HEAD_DIM
KV_WIDTH = N_KV_HEADS * HEAD_DIM
S5_WIDTH = D_MODEL - NSA_WIDTH
S5_GROUP = 16
S5_GROUPS = S5_WIDTH // S5_GROUP
S5_STATE = 64
CMP_LEN = 32
CMP_STRIDE = 16
CMP_HIDDEN = 256
SEL_BLOCK = 64
SEL_TOPK = 16
WINDOW = 512
Q_BLOCK = 128
ROPE_THETA = 10000.0
RMS_EPS = 1e-6
NEG = -1e30
BIG = 1e9
IN_COLS = NSA_WIDTH + 6 * KV_WIDTH + 3 * N_HEADS + S5_WIDTH

kernel_name = "hybrid_nsa_s5_macaron_block"


def rmsnorm(x, g):
    xf = x.astype(jnp.float32)
    y = xf * lax.rsqrt(jnp.mean(xf * xf, axis=-1, keepdims=True) + RMS_EPS)
    return (y * g.astype(jnp.float32)).astype(x.dtype)


def swiglu(x, w1, w3, w2):
    return (jax.nn.silu(x @ w1) * (x @ w3)) @ w2


def rope(x, pos):
    half = HEAD_DIM // 2
    inv = ROPE_THETA ** (-jnp.arange(half, dtype=jnp.float32) / half)
    ang = pos[:, None] * inv[None, :]
    cos = jnp.cos(ang)[None, :, None, :]
    sin = jnp.sin(ang)[None, :, None, :]
    xf = x.astype(jnp.float32)
    x1, x2 = xf[..., :half], xf[..., half:]
    return jnp.concatenate([x1 * cos - x2 * sin, x1 * sin + x2 * cos], axis=-1).astype(x.dtype)


def masked_softmax(s, mask):
    s = jnp.where(mask, s, NEG)
    m = jnp.max(s, axis=-1, keepdims=True)
    e = jnp.where(mask, jnp.exp(s - m), 0.0)
    return e / jnp.maximum(jnp.sum(e, axis=-1, keepdims=True), 1e-30)


def compress(kv, pe, w1, w2):
    B, L = kv.shape[:2]
    n_cmp = (L - CMP_LEN) // CMP_STRIDE + 1
    tok = jnp.arange(n_cmp)[:, None] * CMP_STRIDE + jnp.arange(CMP_LEN)[None, :]
    blk = kv[:, tok] + pe[None, None, :, None, :]
    blk = jnp.transpose(blk, (0, 1, 3, 2, 4)).reshape(B, n_cmp, N_KV_HEADS, CMP_LEN * HEAD_DIM)
    return jax.nn.gelu(blk @ w1) @ w2


def nsa_attention(q, k_cmp, v_cmp, k_slc, v_slc, k_win, v_win, gates, pe_k, pe_v, wk1, wk2, wv1, wv2):
    B, L = q.shape[:2]
    G, HPG, DK = N_KV_HEADS, HEADS_PER_GROUP, HEAD_DIM
    n_cmp = (L - CMP_LEN) // CMP_STRIDE + 1
    n_sel = L // SEL_BLOCK
    n_qb = L // Q_BLOCK
    top_k = min(SEL_TOPK, n_sel)
    scale = DK ** -0.5

    kc = compress(k_cmp, pe_k, wk1, wk2)
    vc = compress(v_cmp, pe_v, wv1, wv2)
    c_start = jnp.arange(n_cmp) * CMP_STRIDE
    c_end = c_start + CMP_LEN - 1
    s_start = jnp.arange(n_sel) * SEL_BLOCK
    overlap = ((c_start[:, None] < s_start[None, :] + SEL_BLOCK)
               & (c_start[:, None] + CMP_LEN > s_start[None, :])).astype(jnp.float32)

    ks_blk = jnp.transpose(k_slc.reshape(B, n_sel, SEL_BLOCK, G, DK), (0, 3, 1, 2, 4))
    vs_blk = jnp.transpose(v_slc.reshape(B, n_sel, SEL_BLOCK, G, DK), (0, 3, 1, 2, 4))
    kw_pad = jnp.pad(k_win, ((0, 0), (WINDOW, 0), (0, 0), (0, 0)))
    vw_pad = jnp.pad(v_win, ((0, 0), (WINDOW, 0), (0, 0), (0, 0)))
    b_ix = jnp.arange(B)[:, None, None, None]
    g_ix = jnp.arange(G)[None, :, None, None]
    blk_ids = jnp.arange(n_sel)

    q_blocks = jnp.transpose(q.reshape(B, n_qb, Q_BLOCK, G, HPG, DK), (1, 0, 2, 3, 4, 5))
    g_blocks = jnp.transpose(gates.reshape(B, n_qb, Q_BLOCK, G, HPG, 3), (1, 0, 2, 3, 4, 5))

    def one_block(args):
        c, qc, gc = args
        t = c * Q_BLOCK + jnp.arange(Q_BLOCK)
        s = jnp.einsum('bqghd,bcgd->bghqc', qc, kc).astype(jnp.float32) * scale
        p_cmp = masked_softmax(s, c_end[None, :] <= t[:, None])
        o_cmp = jnp.einsum('bghqc,bcgd->bqghd', p_cmp.astype(vc.dtype), vc)
        imp = jnp.einsum('bghqc,cs->bgqs', p_cmp, overlap)
        cur = (t // SEL_BLOCK)[:, None]
        valid = blk_ids[None, :] <= cur
        forced = (blk_ids[None, :] == 0) | (blk_ids[None, :] == cur) | (blk_ids[None, :] == cur - 1)
        score = jnp.where(valid & forced, BIG, jnp.where(valid, imp, -BIG))
        _, idx = lax.top_k(score, top_k)
        k_sel = ks_blk[b_ix, g_ix, idx]
        v_sel = vs_blk[b_ix, g_ix, idx].reshape(B, G, Q_BLOCK, top_k * SEL_BLOCK, DK)
        s = jnp.einsum('bqghd,bgqkrd->bghqkr', qc, k_sel).astype(jnp.float32) * scale
        s = s.reshape(B, G, HPG, Q_BLOCK, top_k * SEL_BLOCK)
        kpos = (idx[..., None] * SEL_BLOCK + jnp.arange(SEL_BLOCK)).reshape(B, G, 1, Q_BLOCK, top_k * SEL_BLOCK)
        p_slc = masked_softmax(s, kpos <= t[:, None])
        o_slc = jnp.einsum('bghqn,bgqnd->bqghd', p_slc.astype(v_sel.dtype), v_sel)
        start = c * Q_BLOCK
        k_w = lax.dynamic_slice_in_dim(kw_pad, start, WINDOW + Q_BLOCK, axis=1)
        v_w = lax.dynamic_slice_in_dim(vw_pad, start, WINDOW + Q_BLOCK, axis=1)
        spos = start - WINDOW + jnp.arange(WINDOW + Q_BLOCK)
        dist = t[:, None] - spos[None, :]
        wmask = (spos[None, :] >= 0) & (dist >= 0) & (dist < WINDOW)
        s = jnp.einsum('bqghd,bsgd->bghqs', qc, k_w).astype(jnp.float32) * scale
        p_win = masked_softmax(s, wmask)
        o_win = jnp.einsum('bghqs,bsgd->bqghd', p_win.astype(v_w.dtype), v_w)
        o = gc[..., 0:1] * o_cmp + gc[..., 1:2] * o_slc + gc[..., 2:3] * o_win
        return o.astype(qc.dtype)

    out = lax.map(one_block, (jnp.arange(n_qb), q_blocks, g_blocks))
    return jnp.transpose(out, (1, 0, 2, 3, 4, 5)).reshape(B, L, NSA_WIDTH)


def s5_mixer(u, a_re, a_im, log_dt, b_re, b_im, c_re, c_im, d_skip, w_glu, b_glu):
    B, L = u.shape[:2]
    f32 = jnp.float32
    uf = u.astype(f32).reshape(B, L, S5_GROUPS, S5_GROUP)
    lam = lax.complex(a_re.astype(f32), a_im.astype(f32))
    dt = jnp.exp(log_dt.astype(f32))[:, None]
    a_bar = jnp.exp(lam * dt)
    b_coef = (a_bar - 1.0) / lam
    bu_re = jnp.einsum('blgi,gni->blgn', uf, b_re.astype(f32))
    bu_im = jnp.einsum('blgi,gni->blgn', uf, b_im.astype(f32))
    bu = lax.complex(bu_re, bu_im) * b_coef
    a_seq = jnp.broadcast_to(a_bar, (1, L) + a_bar.shape)

    def combine(e1, e2):
        a1, x1 = e1
        a2, x2 = e2
        return a2 * a1, a2 * x1 + x2

    _, state = lax.associative_scan(combine, (a_seq, bu), axis=1)
    y = (jnp.einsum('blgn,gon->blgo', jnp.real(state), c_re.astype(f32))
         - jnp.einsum('blgn,gon->blgo', jnp.imag(state), c_im.astype(f32))
         + d_skip.astype(f32) * uf)
    y = jax.nn.gelu(y.reshape(B, L, S5_WIDTH))
    out = y * jax.nn.sigmoid(y @ w_glu.astype(f32) + b_glu.astype(f32))
    return out.astype(u.dtype)


def setup_inputs(seed: int = 0) -> dict:
    key = jax.random.key(seed)
    ks = jax.random.split(key, 40)
    f32 = jnp.float32

    def nrm(k, shape, fan_in):
        return jax.random.normal(k, shape, f32) * (fan_in ** -0.5)

    def gain(k, shape):
        return 1.0 + 0.05 * jax.random.normal(k, shape, f32)

    n_idx = jnp.arange(S5_STATE, dtype=f32)
    return {
        "x": jax.random.normal(ks[0], (BATCH, SEQ, D_MODEL), f32),
        "p": jax.random.normal(ks[1], (DEPTH, BATCH, SEQ, PLE_DIM), f32),
        "norm_ffn1": gain(ks[2], (DEPTH, D_MODEL)),
        "ffn1_w1": nrm(ks[3], (DEPTH, D_MODEL, D_FF), D_MODEL),
        "ffn1_w3": nrm(ks[4], (DEPTH, D_MODEL, D_FF), D_MODEL),
        "ffn1_w2": nrm(ks[5], (DEPTH, D_FF, D_MODEL), D_FF),
        "norm_mix": gain(ks[6], (DEPTH, D_MODEL)),
        "w_in": nrm(ks[7], (DEPTH, D_MODEL, IN_COLS), D_MODEL),
        "cmp_pe_k": 0.1 * jax.random.normal(ks[8], (DEPTH, CMP_LEN, HEAD_DIM), f32),
        "cmp_pe_v": 0.1 * jax.random.normal(ks[9], (DEPTH, CMP_LEN, HEAD_DIM), f32),
        "cmp_wk1": nrm(ks[10], (DEPTH, CMP_LEN * HEAD_DIM, CMP_HIDDEN), CMP_LEN * HEAD_DIM),
        "cmp_wk2": nrm(ks[11], (DEPTH, CMP_HIDDEN, HEAD_DIM), CMP_HIDDEN),
        "cmp_wv1": nrm(ks[12], (DEPTH, CMP_LEN * HEAD_DIM, CMP_HIDDEN), CMP_LEN * HEAD_DIM),
        "cmp_wv2": nrm(ks[13], (DEPTH, CMP_HIDDEN, HEAD_DIM), CMP_HIDDEN),
        "s5_a_re": -0.5 + 0.01 * jax.random.normal(ks[14], (DEPTH, S5_GROUPS, S5_STATE), f32),
        "s5_a_im": jnp.pi * n_idx + 0.01 * jax.random.normal(ks[15], (DEPTH, S5_GROUPS, S5_STATE), f32),
        "s5_log_dt": jax.random.uniform(ks[16], (DEPTH, S5_GROUPS), f32, jnp.log(0.001), jnp.log(0.1)),
        "s5_b_re": nrm(ks[17], (DEPTH, S5_GROUPS, S5_STATE, S5_GROUP), 2 * S5_GROUP),
        "s5_b_im": nrm(ks[18], (DEPTH, S5_GROUPS, S5_STATE, S5_GROUP), 2 * S5_GROUP),
        "s5_c_re": nrm(ks[19], (DEPTH, S5_GROUPS, S5_GROUP, S5_STATE), S5_STATE),
        "s5_c_im": nrm(ks[20], (DEPTH, S5_GROUPS, S5_GROUP, S5_STATE), S5_STATE),
        "s5_d": jax.random.normal(ks[21], (DEPTH, S5_GROUPS, S5_GROUP), f32),
        "s5_w_glu": nrm(ks[22], (DEPTH, S5_WIDTH, S5_WIDTH), S5_WIDTH),
        "s5_b_glu": 0.01 * jax.random.normal(ks[23], (DEPTH, S5_WIDTH), f32),
        "w_out": nrm(ks[24], (DEPTH, D_MODEL, D_MODEL), D_MODEL),
        "norm_ffn2": gain(ks[25], (DEPTH, D_MODEL)),
        "ffn2_w1": nrm(ks[26], (DEPTH, D_MODEL, D_FF), D_MODEL),
        "ffn2_w3": nrm(ks[27], (DEPTH, D_MODEL, D_FF), D_MODEL),
        "ffn2_w2": nrm(ks[28], (DEPTH, D_FF, D_MODEL), D_FF),
        "norm_ple": gain(ks[29], (DEPTH, D_MODEL)),
        "w_ple_gate": nrm(ks[30], (DEPTH, D_MODEL, D_MODEL), D_MODEL),
        "w_ple": nrm(ks[31], (DEPTH, PLE_DIM, D_MODEL), PLE_DIM),
        "norm_final": gain(ks[32], (D_MODEL,)),
    }


def reference(x, p, norm_ffn1, ffn1_w1, ffn1_w3, ffn1_w2, norm_mix, w_in,
              cmp_pe_k, cmp_pe_v, cmp_wk1, cmp_wk2, cmp_wv1, cmp_wv2,
              s5_a_re, s5_a_im, s5_log_dt, s5_b_re, s5_b_im, s5_c_re, s5_c_im, s5_d,
              s5_w_glu, s5_b_glu, w_out, norm_ffn2, ffn2_w1, ffn2_w3, ffn2_w2,
              norm_ple, w_ple_gate, w_ple, norm_final):
    B, L = x.shape[:2]
    pos = jnp.arange(L, dtype=jnp.float32)
    split_at = [NSA_WIDTH + k * KV_WIDTH for k in range(7)] + [NSA_WIDTH + 6 * KV_WIDTH + 3 * N_HEADS]
    h = x
    for i in range(DEPTH):
        h = h + 0.5 * swiglu(rmsnorm(h, norm_ffn1[i]), ffn1_w1[i], ffn1_w3[i], ffn1_w2[i])
        z = rmsnorm(h, norm_mix[i]) @ w_in[i]
        q, kc, vc, ksl, vsl, kw, vw, g, u = jnp.split(z, split_at, axis=-1)
        kvshape = (B, L, N_KV_HEADS, HEAD_DIM)
        q = rope(q.reshape(B, L, N_HEADS, HEAD_DIM), pos)
        kc = rope(kc.reshape(kvshape), pos)
        ksl = rope(ksl.reshape(kvshape), pos)
        kw = rope(kw.reshape(kvshape), pos)
        gates = jax.nn.sigmoid(g.reshape(B, L, N_HEADS, 3))
        o_nsa = nsa_attention(q, kc, vc.reshape(kvshape), ksl, vsl.reshape(kvshape), kw, vw.reshape(kvshape),
                              gates, cmp_pe_k[i], cmp_pe_v[i], cmp_wk1[i], cmp_wk2[i], cmp_wv1[i], cmp_wv2[i])
        o_s5 = s5_mixer(u, s5_a_re[i], s5_a_im[i], s5_log_dt[i], s5_b_re[i], s5_b_im[i],
                        s5_c_re[i], s5_c_im[i], s5_d[i], s5_w_glu[i], s5_b_glu[i])
        h = h + jnp.concatenate([o_nsa, o_s5], axis=-1) @ w_out[i]
        h = h + 0.5 * swiglu(rmsnorm(h, norm_ffn2[i]), ffn2_w1[i], ffn2_w3[i], ffn2_w2[i])
        gate = jax.nn.sigmoid(rmsnorm(h, norm_ple[i]) @ w_ple_gate[i])
        h = h + gate * (p[i] @ w_ple[i])
    return rmsnorm(h, norm_final)
```

```python
from contextlib import ExitStack
import math
import numpy as np
import concourse.bass as bass
import concourse.mybir as mybir
from concourse.bass_utils import run_bass_kernel_spmd

F32 = mybir.dt.float32
BF16 = mybir.dt.bfloat16
I32 = mybir.dt.int32
AF = mybir.ActivationFunctionType
ALU = mybir.AluOpType
AX = mybir.AxisListType

D = 1024
DFF = 2816
NFF = DFF // 128
EPS = 1e-6
NEGM = -30000.0
BIG = 1.0e9
SEM_ROT = 8000
DMA_ROT = 500


class Ctx:
    def __init__(self, nc):
        self.nc = nc
        self.es = ExitStack()
        self.eng = {'pe': nc.tensor, 'act': nc.scalar, 'dve': nc.vector, 'pool': nc.gpsimd, 'sp': nc.sync}
        self.nsem = 0
        self.csem = {}
        self.cseq = {}
        for e in ('pe', 'act', 'dve', 'pool'):
            self.csem[e] = self.new_sem()
            self.cseq[e] = 0
        self.dsl = {q: [[self.new_sem(), 0] for _ in range(4)] for q in ('sp', 'pool', 'act')}
        self.dnext = {q: 0 for q in self.dsl}
        self.waited = {e: {} for e in self.eng}
        self.bufs = {}
        self.latest = {}

    def new_sem(self):
        self.nsem += 1
        return self.es.enter_context(self.nc.semaphore("s%d" % self.nsem))

    def _wait(self, e, ev):
        if ev is None:
            return
        sem, val = ev
        w = self.waited[e]
        if w.get(sem, 0) >= val:
            return
        self.eng[e].wait_ge(sem, val)
        w[sem] = val

    def _deps(self, e, reads, writes, pe_acc):
        evs = []
        for k in reads:
            b = self.bufs.get(k)
            if b is not None and b[0]:
                evs.extend(b[0])
        for k in writes:
            b = self.bufs.get(k)
            if b is None:
                continue
            if b[0]:
                if e in self.dsl and b[2] in self.dsl and not b[1]:
                    pass
                elif not (pe_acc and b[2] == 'pe' and not b[1]):
                    evs.extend(b[0])
            evs.extend(b[1])
        for ev in evs:
            self._wait(e, ev)

    def _record(self, e, ev, reads, writes):
        self.latest[ev[0]] = ev[1]
        for k in reads:
            b = self.bufs.setdefault(k, [[], [], None])
            b[1].append(ev)
        for k in writes:
            b = self.bufs.get(k)
            if b is not None and e in self.dsl and b[2] in self.dsl and not b[1] and k not in reads:
                b[0].append(ev)
            else:
                self.bufs[k] = [[ev], [], e]

    def op(self, e, fn, reads=(), writes=(), pe_acc=False):
        self._deps(e, reads, writes, pe_acc)
        ins = fn(self.eng[e])
        if self.cseq[e] >= SEM_ROT:
            self.csem[e] = self.new_sem()
            self.cseq[e] = 0
        self.cseq[e] += 1
        ins.then_inc(self.csem[e], 1)
        self._record(e, (self.csem[e], self.cseq[e]), reads, writes)

    def dma(self, q, out, in_, reads=(), writes=()):
        sl = self.dsl[q]
        i = self.dnext[q]
        self.dnext[q] = (i + 1) % len(sl)
        if sl[i][1] >= DMA_ROT:
            self._wait(q, (sl[i][0], 16 * sl[i][1]))
            sl[i] = [self.new_sem(), 0]
        sem, cnt = sl[i]
        if cnt > 0:
            self._wait(q, (sem, 16 * cnt))
        self._deps(q, reads, writes, False)
        ins = self.eng[q].dma_start(out=out, in_=in_)
        ins.then_inc(sem, 16)
        sl[i][1] = cnt + 1
        self._record(q, (sem, 16 * (cnt + 1)), reads, writes)

    def barrier(self):
        for e in self.eng:
            for sem, val in list(self.latest.items()):
                self._wait(e, (sem, val))
        self.bufs = {}


def build(L, stop_after=99, debug=False):
    NB = L // 128
    NBo = NB // 2
    Lo = L // 2
    NCMP = L // 16 - 1
    NCC = max(1, L // 2048)
    nc = bass.Bass("TRN2", target_bir_lowering=False)
    c = Ctx(nc)

    def din(name, shape, dt=F32):
        return nc.dram_tensor(name, list(shape), dt, kind="ExternalInput").ap()

    def dscr(name, shape, dt):
        return nc.dram_tensor(name, list(shape), dt, kind=("ExternalOutput" if debug else "Internal")).ap()

    par = din("par", [128, 2])
    xT = din("xT", [D, L])
    pT = din("pT", [256, Lo])
    gains = din("gains", [5, 128, 8])
    f1w1 = din("f1w1", [D, DFF]); f1w3 = din("f1w3", [D, DFF]); f1w2 = din("f1w2", [DFF, D])
    f2w1 = din("f2w1", [D, DFF]); f2w3 = din("f2w3", [D, DFF]); f2w2 = din("f2w2", [DFF, D])
    winp = din("winp", [D, 1816]); wins = din("wins", [D, 896])
    ropec = din("ropec", [128, L]); ropes = din("ropes", [128, L])
    peT = din("peT", [2, 64, 32])
    cw1 = din("cw1", [2, 2048, 256]); cw2p = din("cw2p", [2, 2, 256, 128])
    cw2 = din("cw2", [2, 256, 64])
    s5p = din("s5p", [3, 128, 16])
    s5B = din("s5B", [2, 16, 128, 128])
    s5C = din("s5C", [2, 16, 128, 128])
    s5d = din("s5d", [2, 128, 4])
    wglu = din("wglu", [512, 512])
    wout = din("wout", [D, D]); wgate = din("wgate", [D, D]); wple = din("wple", [256, D])
    wmask = din("wmask", [128, 6, 512])
    vis8 = din("vis8", [128, 8, 128])
    vfrel = din("vfrel", [128, 2, 256])
    ovl = din("ovl", [NCC, 128, 128])
    indneg = din("indneg", [128, L])
    outT = nc.dram_tensor("outT", [D, Lo], F32, kind="ExternalOutput").ap()

    h1_s = dscr("h1_s", [D, L], F32)
    h2_s = dscr("h2_s", [D, Lo], F32)
    h3_s = dscr("h3_s", [D, Lo], F32)
    q_s = dscr("q_s", [128, NB, 4, 128], BF16)
    kc_s = dscr("kc_s", [128, L], BF16); vc_s = dscr("vc_s", [128, L], BF16)
    ksl_s = dscr("ksl_s", [128, L], BF16); kw_s = dscr("kw_s", [128, L], BF16)
    vsl_t = dscr("vsl_t", [L, 128], BF16); vw_t = dscr("vw_t", [L, 128], BF16)
    g_s = dscr("g_s", [24, L], F32)
    u_s = dscr("u_s", [512, L], BF16)
    os5_s = dscr("os5_s", [512, L], BF16)
    onsa_s = dscr("onsa_s", [512, Lo], BF16)

    def ch(ap):
        return ap.rearrange("(c p) n -> p c n", p=128)

    with ExitStack() as g0:
        sb = lambda es, name, shape, dt: es.enter_context(nc.sbuf_tensor(name, list(shape), dt))
        PS = [g0.enter_context(nc.psum_tensor("ps%d" % i, [128, 512], F32)) for i in range(8)]
        ones_bf = sb(g0, "ones_bf", [128, 128], BF16)
        ones_f = sb(g0, "ones_f", [128, 128], F32)
        id_f = sb(g0, "id_f", [128, 128], F32)
        id_bf = sb(g0, "id_bf", [128, 128], BF16)
        gn = sb(g0, "gn", [128, 5, 8], F32)
        c.op('dve', lambda e: e.memset(ones_bf[:], 1.0), writes=['ones_bf'])
        c.op('dve', lambda e: e.memset(ones_f[:], 1.0), writes=['ones_f'])
        c.op('pool', lambda e: e.affine_select(out=id_f[:], in_=ones_f[:], pattern=[[-1, 128]], compare_op=ALU.is_equal,
                                               fill=0.0, base=0, channel_multiplier=1), reads=['ones_f'], writes=['id_f'])
        c.op('dve', lambda e: e.tensor_copy(out=id_bf[:], in_=id_f[:]), reads=['id_f'], writes=['id_bf'])
        parv = sb(g0, "parv", [128, 2], F32)
        for i in range(5):
            c.dma('sp', gn[:, i, :], gains[i], writes=['gn'])
        c.dma('sp', parv[:], par[:, :], writes=['parv'])

        def blend(eng, out, ev, od, rd, wr, p0=0, p1=128):
            eng = 'dve'
            c.op(eng, lambda e: e.tensor_scalar(out=out, in0=od, scalar1=parv[p0:p1, 0:1], scalar2=None, op0=ALU.mult),
                 reads=list(rd) + ['parv'], writes=wr)
            c.op(eng, lambda e: e.scalar_tensor_tensor(out=out, in0=ev, scalar=parv[p0:p1, 1:2], in1=out, op0=ALU.mult, op1=ALU.add),
                 reads=list(rd) + ['parv'] + list(wr), writes=wr)

        def rmsnorm(es_tiles, xt, xk, gi, T, a_out, a_key, ps_i):
            sq, rstd = es_tiles
            c.op('act', lambda e: e.activation(out=sq[:, 0:8, 0:T], in_=xt, func=AF.Square), reads=[xk], writes=['sq'])
            for kc in range(8):
                c.op('pe', lambda e, kc=kc: e.matmul(PS[ps_i][:, 0:T], ones_bf[:], sq[:, kc, 0:T], start=(kc == 0), stop=(kc == 7)),
                     reads=['sq', 'ones_bf'], writes=[('ps', ps_i)], pe_acc=True)
            c.op('act', lambda e: e.activation(out=rstd[:, 0:T], in_=PS[ps_i][:, 0:T], func=AF.Sqrt, bias=EPS, scale=1.0 / D),
                 reads=[('ps', ps_i)], writes=['rstd'])
            c.op('dve', lambda e: e.reciprocal(out=rstd[:, 0:T], in_=rstd[:, 0:T]), reads=['rstd'], writes=['rstd'])
            for kc in range(8):
                c.op('dve', lambda e, kc=kc: e.scalar_tensor_tensor(out=a_out[:, kc, 0:T], in0=xt[:, kc, :], scalar=gn[:, gi, kc:kc + 1],
                                                                     in1=rstd[:, 0:T], op0=ALU.mult, op1=ALU.mult),
                     reads=[xk, 'gn', 'rstd'], writes=[a_key])

        def load_w(es, name, src, kchunks, ncols):
            t = sb(es, name, [128, kchunks, ncols], BF16)
            for kc in range(kchunks):
                c.dma('pool', t[:, kc, :], src[kc * 128:(kc + 1) * 128, :], writes=[name])
            return t

        def ffn_phase(src, dst, ntok, w1d, w3d, w2d, gi, tag):
            T = 512
            with ExitStack() as es:
                w1 = load_w(es, "w1" + tag, w1d, 8, DFF)
                w3 = load_w(es, "w3" + tag, w3d, 8, DFF)
                w2 = load_w(es, "w2" + tag, w2d, NFF, D)
                xts = [sb(es, "xt%d%s" % (i, tag), [128, 8, T], F32) for i in range(2)]
                sq = sb(es, "sq" + tag, [128, NFF, T], BF16)
                a = sb(es, "a" + tag, [128, 8, T], BF16)
                rstd = sb(es, "rstd" + tag, [128, T], F32)
                s1 = [sb(es, "s1%d%s" % (i, tag), [128, T], F32) for i in range(2)]
                srcv, dstv = ch(src), ch(dst)
                nt = ntok // T
                c.dma('sp', xts[0][:], srcv[:, :, 0:T], writes=[('xt', 0)])
                for it in range(nt):
                    xt, xk = xts[it % 2], ('xt', it % 2)
                    if it + 1 < nt:
                        c.dma('sp', xts[(it + 1) % 2][:], srcv[:, :, (it + 1) * T:(it + 2) * T], writes=[('xt', (it + 1) % 2)])
                    rmsnorm((sq, rstd), xt[:], xk, gi, T, a, 'a', 6)
                    for m in range(NFF):
                        p1, p3 = m % 2, 2 + m % 2
                        for kc in range(8):
                            c.op('pe', lambda e, kc=kc, m=m, p1=p1: e.matmul(PS[p1][:, 0:T], w1[:, kc, m * 128:(m + 1) * 128], a[:, kc, :],
                                                                               start=(kc == 0), stop=(kc == 7)),
                                 reads=['a', 'w1' + tag], writes=[('ps', p1)], pe_acc=True)
                        for kc in range(8):
                            c.op('pe', lambda e, kc=kc, m=m, p3=p3: e.matmul(PS[p3][:, 0:T], w3[:, kc, m * 128:(m + 1) * 128], a[:, kc, :],
                                                                               start=(kc == 0), stop=(kc == 7)),
                                 reads=['a', 'w3' + tag], writes=[('ps', p3)], pe_acc=True)
                        c.op('act', lambda e, m=m, p1=p1: e.activation(out=s1[m % 2][:], in_=PS[p1][:, 0:T], func=AF.Silu),
                             reads=[('ps', p1)], writes=[('s1', m % 2)])
                        c.op('dve', lambda e, m=m, p3=p3: e.tensor_tensor(out=sq[:, m, :], in0=PS[p3][:, 0:T], in1=s1[m % 2][:], op=ALU.mult),
                             reads=[('ps', p3), ('s1', m % 2)], writes=[('g', m)])
                    for mo in range(8):
                        po = 4 + mo % 2
                        for m in range(NFF):
                            c.op('pe', lambda e, m=m, mo=mo, po=po: e.matmul(PS[po][:, 0:T], w2[:, m, mo * 128:(mo + 1) * 128], sq[:, m, :],
                                                                               start=(m == 0), stop=(m == NFF - 1)),
                                 reads=[('g', m), 'sq', 'w2' + tag], writes=[('ps', po)], pe_acc=True)
                        c.op('dve', lambda e, mo=mo, po=po: e.scalar_tensor_tensor(out=xt[:, mo, :], in0=PS[po][:, 0:T], scalar=0.5, in1=xt[:, mo, :],
                                                                                     op0=ALU.mult, op1=ALU.add),
                             reads=[('ps', po), xk], writes=[xk])
                    c.dma('act', dstv[:, :, it * T:(it + 1) * T], xt[:], reads=[xk])
            c.barrier()

        ffn_phase(xT, h1_s, L, f1w1, f1w3, f1w2, 0, "A")
        if stop_after <= 1:
            c.es.close()
            return nc

        with ExitStack() as es:
            T = 512
            wp = load_w(es, "wp", winp, 8, 1816)
            ws = load_w(es, "ws", wins, 8, 896)
            xts = [sb(es, "p2x%d" % i, [128, 8, T], F32) for i in range(2)]
            sq = sb(es, "p2sq", [128, 8, T], BF16)
            a = sb(es, "p2a", [128, 8, T], BF16)
            rstd = sb(es, "p2r", [128, T], F32)
            rc = [sb(es, "p2c%d" % i, [128, T], F32) for i in range(2)]
            rs = [sb(es, "p2s%d" % i, [128, T], F32) for i in range(2)]
            t1 = [sb(es, "p2t1%d" % i, [128, T], F32) for i in range(2)]
            t2 = [sb(es, "p2t2%d" % i, [128, T], F32) for i in range(2)]
            ob = [sb(es, "p2o%d" % i, [128, T], BF16) for i in range(3)]
            gb = [sb(es, "p2g%d" % i, [24, T], F32) for i in range(2)]
            h1v = ch(h1_s)
            nt = L // T
            cnt = [0, 0]
            c.dma('sp', xts[0][:], h1v[:, :, 0:T], writes=[('xt', 0)])
            for it in range(nt):
                xt, xk = xts[it % 2], ('xt', it % 2)
                t0 = it * T
                if it + 1 < nt:
                    c.dma('sp', xts[(it + 1) % 2][:], h1v[:, :, (it + 1) * T:(it + 2) * T], writes=[('xt', (it + 1) % 2)])
                ri = it % 2
                c.dma('sp', rc[ri][:], ropec[:, t0:t0 + T], writes=[('rc', ri)])
                c.dma('sp', rs[ri][:], ropes[:, t0:t0 + T], writes=[('rs', ri)])
                rmsnorm((sq, rstd), xt[:], xk, 1, T, a, 'a', 6)

                def proj(pcols, rope, dst_ap, ob_view=None):
                    i = cnt[0] % 2
                    cnt[0] += 1
                    pz, pzs = i, 2 + i
                    for kc in range(8):
                        c.op('pe', lambda e, kc=kc: e.matmul(PS[pz][:, 0:T], wp[:, kc, pcols:pcols + 128], a[:, kc, :],
                                                             start=(kc == 0), stop=(kc == 7)),
                             reads=['a', 'wp'], writes=[('ps', pz)], pe_acc=True)
                    oi = cnt[1] % 3
                    cnt[1] += 1
                    o, ok = ob[oi], ('ob', oi)
                    if rope:
                        for kc in range(8):
                            c.op('pe', lambda e, kc=kc: e.matmul(PS[pzs][:, 0:T], ws[:, kc, pcols:pcols + 128], a[:, kc, :],
                                                                 start=(kc == 0), stop=(kc == 7)),
                                 reads=['a', 'ws'], writes=[('ps', pzs)], pe_acc=True)
                        c.op('dve', lambda e: e.tensor_tensor(out=t1[i][:], in0=PS[pz][:, 0:T], in1=rc[ri][:], op=ALU.mult),
                             reads=[('ps', pz), ('rc', ri)], writes=[('t1', i)])
                        c.op('dve', lambda e: e.tensor_tensor(out=t2[i][:], in0=PS[pzs][:, 0:T], in1=rs[ri][:], op=ALU.mult),
                             reads=[('ps', pzs), ('rs', ri)], writes=[('t2', i)])
                        c.op('dve', lambda e: e.tensor_tensor(out=o[:], in0=t1[i][:], in1=t2[i][:], op=ALU.add),
                             reads=[('t1', i), ('t2', i)], writes=[ok])
                    else:
                        c.op('act', lambda e: e.activation(out=o[:], in_=PS[pz][:, 0:T], func=AF.Copy), reads=[('ps', pz)], writes=[ok])
                    src = o[:] if ob_view is None else ob_view(o)
                    c.dma('sp', dst_ap, src, reads=[ok])

                proj(512, True, kc_s[:, t0:t0 + T])
                proj(640, True, ksl_s[:, t0:t0 + T])
                proj(768, True, kw_s[:, t0:t0 + T])
                proj(896, False, vc_s[:, t0:t0 + T])
                for uc in range(4):
                    proj(1280 + uc * 128, False, u_s[uc * 128:(uc + 1) * 128, t0:t0 + T])
                for h in range(4):
                    proj(h * 128, True, q_s[:, it * 4:(it + 1) * 4, h, :], ob_view=lambda o: o[:].rearrange("p (b q) -> p b q", b=4))
                i = cnt[0] % 2
                cnt[0] += 1
                pz = i
                for kc in range(8):
                    c.op('pe', lambda e, kc=kc: e.matmul(PS[pz][0:24, 0:T], wp[:, kc, 1792:1816], a[:, kc, :], start=(kc == 0), stop=(kc == 7)),
                         reads=['a', 'wp'], writes=[('ps', pz)], pe_acc=True)
                c.op('act', lambda e: e.activation(out=gb[ri][:], in_=PS[pz][0:24, 0:T], func=AF.Sigmoid), reads=[('ps', pz)], writes=[('gb', ri)])
                c.dma('sp', g_s[:, t0:t0 + T], gb[ri][:], reads=[('gb', ri)])
                for tb in range(4):
                    i = cnt[0] % 2
                    cnt[0] += 1
                    pz = i
                    for kc in range(8):
                        c.op('pe', lambda e, kc=kc: e.matmul(PS[pz][:, 0:256], a[:, kc, tb * 128:(tb + 1) * 128], wp[:, kc, 1024:1280],
                                                             start=(kc == 0), stop=(kc == 7)),
                             reads=['a', 'wp'], writes=[('ps', pz)], pe_acc=True)
                    oi = cnt[1] % 3
                    cnt[1] += 1
                    o, ok = ob[oi], ('ob', oi)
                    c.op('act', lambda e: e.activation(out=o[:, 0:256], in_=PS[pz][:, 0:256], func=AF.Copy), reads=[('ps', pz)], writes=[ok])
                    r0 = t0 + tb * 128
                    c.dma('sp', vsl_t[r0:r0 + 128, :], o[:, 0:128], reads=[ok])
                    c.dma('sp', vw_t[r0:r0 + 128, :], o[:, 128:256], reads=[ok])
        c.barrier()

        if stop_after <= 2:
            c.es.close()
            return nc
        with ExitStack() as es:
            TS = 512
            NL = TS.bit_length() - 1
            prm = sb(es, "s5prm", [128, 3, 16], F32)
            wk = {n: sb(es, "s5k_" + n, [128, 16], F32) for n in
                  ("dt", "rre", "th", "rmag", "y", "yf", "f", "adj", "ang", "sn", "cs", "abr", "abi", "den", "t1", "t2", "bcr", "bci")}
            yi = sb(es, "s5yi", [128, 16], I32)
            cosT = sb(es, "s5cos", [128, 16, TS], F32)
            sinT = sb(es, "s5sin", [128, 16, TS], F32)
            tmpd = [sb(es, "s5tmp%d" % i, [128, TS], F32) for i in range(2)]
            blt = [sb(es, "s5bl%d" % i, [128, 128], F32) for i in range(4)]
            bT = [sb(es, "s5bT%d" % i, [128, 16, 128], BF16) for i in range(2)]
            cT = [sb(es, "s5cT%d" % i, [128, 16, 128], BF16) for i in range(2)]
            dsk = sb(es, "s5dsk", [128, 2, 4], F32)
            wg = load_w(es, "wglusb", wglu, 4, 512)
            xe = [sb(es, "s5xe%d" % i, [128, 16], F32) for i in range(2)]
            for i in range(3):
                c.dma('sp', prm[:, i, :], s5p[i], writes=['prm'])
            for i in range(2):
                c.dma('sp', dsk[:, i, :], s5d[i], writes=['dsk'])
                c.op('dve', lambda e, i=i: e.memset(xe[i][:], 0.0), writes=[('xe', i, s) for s in range(16)])
                for s in range(16):
                    c.dma('pool', cT[i][:, s, :], s5C[i, s], writes=['cT'])
            P = lambda n: wk[n][:]

            def dv(fn, rd, wr):
                c.op('dve', fn, reads=['prm'] + rd, writes=wr)
            c.op('act', lambda e: e.activation(out=P("dt"), in_=prm[:, 2, :], func=AF.Exp), reads=['prm'], writes=['dt'])
            dv(lambda e: e.tensor_tensor(out=P("rre"), in0=prm[:, 0, :], in1=P("dt"), op=ALU.mult), ['dt'], ['rre'])
            dv(lambda e: e.tensor_tensor(out=P("th"), in0=prm[:, 1, :], in1=P("dt"), op=ALU.mult), ['dt'], ['th'])
            c.op('act', lambda e: e.activation(out=P("rmag"), in_=P("rre"), func=AF.Exp), reads=['rre'], writes=['rmag'])
            dv(lambda e: e.tensor_scalar(out=P("y"), in0=P("th"), scalar1=1.0 / (2 * math.pi), scalar2=None, op0=ALU.mult), ['th'], ['y'])
            dv(lambda e: e.tensor_copy(out=yi[:], in_=P("y")), ['y'], ['yi'])
            dv(lambda e: e.tensor_copy(out=P("yf"), in_=yi[:]), ['yi'], ['yf'])
            dv(lambda e: e.tensor_tensor(out=P("f"), in0=P("y"), in1=P("yf"), op=ALU.subtract), ['y', 'yf'], ['f'])
            dv(lambda e: e.tensor_scalar(out=P("adj"), in0=P("f"), scalar1=0.5, scalar2=None, op0=ALU.is_gt), ['f'], ['adj'])
            dv(lambda e: e.tensor_tensor(out=P("f"), in0=P("f"), in1=P("adj"), op=ALU.subtract), ['f', 'adj'], ['f'])
            dv(lambda e: e.tensor_scalar(out=P("adj"), in0=P("f"), scalar1=-0.5, scalar2=None, op0=ALU.is_lt), ['f'], ['adj'])
            dv(lambda e: e.tensor_tensor(out=P("f"), in0=P("f"), in1=P("adj"), op=ALU.add), ['f', 'adj'], ['f'])
            dv(lambda e: e.tensor_scalar(out=P("ang"), in0=P("f"), scalar1=6.283185, scalar2=None, op0=ALU.mult), ['f'], ['ang'])
            c.op('act', lambda e: e.activation(out=P("sn"), in_=P("ang"), func=AF.Sin), reads=['ang'], writes=['sn'])
            dv(lambda e: e.tensor_scalar(out=P("t1"), in0=P("ang"), scalar1=-1.0, scalar2=None, op0=ALU.mult), ['ang'], ['t1'])
            dv(lambda e: e.tensor_tensor(out=P("t1"), in0=P("t1"), in1=P("ang"), op=ALU.max), ['ang', 't1'], ['t1'])
            dv(lambda e: e.tensor_scalar(out=P("t1"), in0=P("t1"), scalar1=-1.0, scalar2=1.5707963, op0=ALU.mult, op1=ALU.add), ['t1'], ['t1'])
            c.op('act', lambda e: e.activation(out=P("cs"), in_=P("t1"), func=AF.Sin), reads=['t1'], writes=['cs'])
            dv(lambda e: e.tensor_tensor(out=P("abr"), in0=P("rmag"), in1=P("cs"), op=ALU.mult), ['rmag', 'cs'], ['abr'])
            dv(lambda e: e.tensor_tensor(out=P("abi"), in0=P("rmag"), in1=P("sn"), op=ALU.mult), ['rmag', 'sn'], ['abi'])
            dv(lambda e: e.tensor_scalar(out=P("abr"), in0=P("abr"), scalar1=-1.0, scalar2=None, op0=ALU.add), ['abr'], ['abr'])
            dv(lambda e: e.tensor_tensor(out=P("den"), in0=prm[:, 0, :], in1=prm[:, 0, :], op=ALU.mult), [], ['den'])
            dv(lambda e: e.tensor_tensor(out=P("t1"), in0=prm[:, 1, :], in1=prm[:, 1, :], op=ALU.mult), ['t1'], ['t1'])
            dv(lambda e: e.tensor_tensor(out=P("den"), in0=P("den"), in1=P("t1"), op=ALU.add), ['den', 't1'], ['den'])
            dv(lambda e: e.reciprocal(out=P("den"), in_=P("den")), ['den'], ['den'])
            dv(lambda e: e.tensor_tensor(out=P("t1"), in0=P("abr"), in1=prm[:, 0, :], op=ALU.mult), ['abr', 't1'], ['t1'])
            dv(lambda e: e.tensor_tensor(out=P("t2"), in0=P("abi"), in1=prm[:, 1, :], op=ALU.mult), ['abi'], ['t2'])
            dv(lambda e: e.tensor_tensor(out=P("t1"), in0=P("t1"), in1=P("t2"), op=ALU.add), ['t1', 't2'], ['t1'])
            dv(lambda e: e.tensor_tensor(out=P("bcr"), in0=P("t1"), in1=P("den"), op=ALU.mult), ['t1', 'den'], ['bcr'])
            dv(lambda e: e.tensor_tensor(out=P("t1"), in0=P("abi"), in1=prm[:, 0, :], op=ALU.mult), ['abi', 't1'], ['t1'])
            dv(lambda e: e.tensor_tensor(out=P("t2"), in0=P("abr"), in1=prm[:, 1, :], op=ALU.mult), ['abr', 't2'], ['t2'])
            dv(lambda e: e.tensor_tensor(out=P("t1"), in0=P("t1"), in1=P("t2"), op=ALU.subtract), ['t1', 't2'], ['t1'])
            dv(lambda e: e.tensor_tensor(out=P("bci"), in0=P("t1"), in1=P("den"), op=ALU.mult), ['t1', 'den'], ['bci'])
            for s in range(16):
                en = 'dve'
                tm = tmpd[s % 2]
                ck, sk, tk = ('cos', s), ('sin', s), ('tmpd', s % 2)
                c.op(en, lambda e, s=s: e.tensor_copy(out=cosT[:, s, 0:1], in_=wk["cs"][:, s:s + 1]), reads=['cs'], writes=[ck])
                c.op(en, lambda e, s=s: e.tensor_copy(out=sinT[:, s, 0:1], in_=wk["sn"][:, s:s + 1]), reads=['sn'], writes=[sk])
                m = 1
                while m < TS:
                    cr, sr = cosT[:, s, m - 1:m], sinT[:, s, m - 1:m]
                    c.op(en, lambda e, s=s, m=m, sr=sr: e.tensor_scalar(out=tm[:, 0:m], in0=sinT[:, s, 0:m], scalar1=sr, scalar2=None, op0=ALU.mult),
                         reads=[sk], writes=[tk])
                    c.op(en, lambda e, s=s, m=m, sr=sr: e.tensor_scalar(out=tm[:, m:2 * m], in0=cosT[:, s, 0:m], scalar1=sr, scalar2=None, op0=ALU.mult),
                         reads=[ck, sk], writes=[tk])
                    c.op(en, lambda e, s=s, m=m, cr=cr: e.scalar_tensor_tensor(out=cosT[:, s, m:2 * m], in0=cosT[:, s, 0:m], scalar=cr, in1=tm[:, 0:m],
                                                                               op0=ALU.mult, op1=ALU.subtract), reads=[ck, tk], writes=[ck])
                    c.op(en, lambda e, s=s, m=m, cr=cr: e.scalar_tensor_tensor(out=sinT[:, s, m:2 * m], in0=sinT[:, s, 0:m], scalar=cr, in1=tm[:, m:2 * m],
                                                                               op0=ALU.mult, op1=ALU.add), reads=[ck, sk, tk], writes=[sk])
                    m *= 2
            for s in range(16):
                bre, bim, bpr, bpi = blt
                c.dma('sp', bre[:], s5B[0, s], writes=['bre'])
                c.dma('sp', bim[:], s5B[1, s], writes=['bim'])
                bcr, bci = wk["bcr"][:, s:s + 1], wk["bci"][:, s:s + 1]
                c.op('dve', lambda e: e.tensor_scalar(out=bpr[:], in0=bim[:], scalar1=bci, scalar2=None, op0=ALU.mult), reads=['bim', 'bci'], writes=['bpr'])
                c.op('dve', lambda e: e.scalar_tensor_tensor(out=bpr[:], in0=bre[:], scalar=bcr, in1=bpr[:], op0=ALU.mult, op1=ALU.subtract),
                     reads=['bre', 'bcr', 'bpr'], writes=['bpr'])
                c.op('dve', lambda e: e.tensor_scalar(out=bpi[:], in0=bre[:], scalar1=bci, scalar2=None, op0=ALU.mult), reads=['bre', 'bci'], writes=['bpi'])
                c.op('dve', lambda e: e.scalar_tensor_tensor(out=bpi[:], in0=bim[:], scalar=bcr, in1=bpi[:], op0=ALU.mult, op1=ALU.add),
                     reads=['bim', 'bcr', 'bpi'], writes=['bpi'])
                for i, (bp, bk) in enumerate(((bpr, 'bpr'), (bpi, 'bpi'))):
                    c.op('pe', lambda e, bp=bp: e.transpose(PS[i][:, 0:128], bp[:], id_f[:]), reads=[bk, 'id_f'], writes=[('ps', i)])
                    c.op('act', lambda e, i=i, s=s: e.activation(out=bT[i][:, s, :], in_=PS[i][:, 0:128], func=AF.Copy), reads=[('ps', i)], writes=['bT'])

            ub = [sb(es, "s5u%d" % i, [128, 4, TS], BF16) for i in range(2)]
            bur = [sb(es, "s5bur%d" % i, [128, TS], F32) for i in range(2)]
            bui = [sb(es, "s5bui%d" % i, [128, TS], F32) for i in range(2)]
            mm = [sb(es, "s5m%d" % i, [128, TS], F32) for i in range(8)]
            mp = [sb(es, "s5mp%d" % i, [128, TS], F32) for i in range(8)]
            rr = [sb(es, "s5rr%d" % i, [128, TS], F32) for i in range(2)]
            rim = [sb(es, "s5ri%d" % i, [128, TS], F32) for i in range(2)]
            wre = [sb(es, "s5wr%d" % i, [128, TS], F32) for i in range(2)]
            wim = [sb(es, "s5wi%d" % i, [128, TS], F32) for i in range(2)]
            xrb = [sb(es, "s5xr%d" % i, [128, TS], BF16) for i in range(2)]
            xib = [sb(es, "s5xi%d" % i, [128, TS], BF16) for i in range(2)]
            tn = sb(es, "s5tn", [128, 4], F32)
            yv = sb(es, "s5yv", [128, 4, TS], F32)
            yt = sb(es, "s5yt", [128, 4, TS], F32)
            ygb = sb(es, "s5ygb", [128, 4, TS], BF16)
            sg = sb(es, "s5sg", [128, TS], F32)
            osb = [sb(es, "s5o%d" % i, [128, 4, TS], BF16) for i in range(2)]
            uv = ch(u_s)
            ov = ch(os5_s)
            nchunk = L // TS
            c.dma('sp', ub[0][:], uv[:, :, 0:TS], writes=[('ub', 0)])

            def stageA(ic, s):
                u, uk = ub[ic % 2], ('ub', ic % 2)
                kc, b = s // 4, s % 2
                c.op('pe', lambda e: e.matmul(PS[b][:, 0:TS], bT[0][:, s, :], u[:, kc, :], start=True, stop=True), reads=['bT', uk], writes=[('ps', b)])
                c.op('pe', lambda e: e.matmul(PS[2 + b][:, 0:TS], bT[1][:, s, :], u[:, kc, :], start=True, stop=True), reads=['bT', uk], writes=[('ps', 2 + b)])
                c.op('act', lambda e: e.activation(out=bur[b][:], in_=PS[b][:, 0:TS], func=AF.Copy), reads=[('ps', b)], writes=[('bur', b)])
                c.op('act', lambda e: e.activation(out=bui[b][:], in_=PS[2 + b][:, 0:TS], func=AF.Copy), reads=[('ps', 2 + b)], writes=[('bui', b)])

            def stageBpre(ic, s):
                kc, b = s // 4, s % 2
                ck, sk = ('cos', s), ('sin', s)
                cs_, sn_ = cosT[:, s, :], sinT[:, s, :]
                M = [mm[4 * b + i] for i in range(4)]
                MK = [('mm', 4 * b + i) for i in range(4)]
                c.op('dve', lambda e: e.tensor_tensor(out=M[0][:], in0=bur[b][:], in1=cs_, op=ALU.mult), reads=[('bur', b), ck], writes=[MK[0]])
                c.op('dve', lambda e: e.tensor_tensor(out=M[1][:], in0=bui[b][:], in1=sn_, op=ALU.mult), reads=[('bui', b), sk], writes=[MK[1]])
                c.op('dve', lambda e: e.tensor_tensor(out=rr[b][:], in0=M[0][:], in1=M[1][:], op=ALU.add), reads=[MK[0], MK[1]], writes=[('rr', b)])
                c.op('pool', lambda e: e.tensor_tensor(out=M[2][:], in0=bui[b][:], in1=cs_, op=ALU.mult), reads=[('bui', b), ck], writes=[MK[2]])
                c.op('pool', lambda e: e.tensor_tensor(out=M[3][:], in0=bur[b][:], in1=sn_, op=ALU.mult), reads=[('bur', b), sk], writes=[MK[3]])

            def stageBrim(ic, s):
                b = s % 2
                M = [mm[4 * b + i] for i in range(4)]
                MK = [('mm', 4 * b + i) for i in range(4)]
                c.op('dve', lambda e: e.tensor_tensor(out=rim[b][:], in0=M[2][:], in1=M[3][:], op=ALU.subtract), reads=[MK[2], MK[3]], writes=[('ri', b)])

            def stageB(ic, s):
                kc, b = s // 4, s % 2
                ck, sk = ('cos', s), ('sin', s)
                cs_, sn_ = cosT[:, s, :], sinT[:, s, :]
                M = [mp[4 * b + i] for i in range(4)]
                MK = [('mp', 4 * b + i) for i in range(4)]
                rb = wk["rmag"][:, s:s + 1].to_broadcast([128, TS])
                c.op('dve', lambda e: e.tensor_tensor_scan(out=wre[b][:], data0=rb, data1=rr[b][:], initial=xe[0][:, s:s + 1], op0=ALU.mult, op1=ALU.add),
                     reads=['rmag', ('rr', b), ('xe', 0, s)], writes=[('wre', b)])
                c.op('dve', lambda e: e.tensor_tensor_scan(out=wim[b][:], data0=rb, data1=rim[b][:], initial=xe[1][:, s:s + 1], op0=ALU.mult, op1=ALU.add),
                     reads=['rmag', ('ri', b), ('xe', 1, s)], writes=[('wim', b)])
                cl, sl_ = cosT[:, s, TS - 1:TS], sinT[:, s, TS - 1:TS]
                wrl, wil = wre[b][:, TS - 1:TS], wim[b][:, TS - 1:TS]
                c.op('dve', lambda e: e.tensor_tensor(out=tn[:, 0:1], in0=wil, in1=sl_, op=ALU.mult), reads=[('wim', b), sk], writes=['tn0'])
                c.op('dve', lambda e: e.tensor_tensor(out=tn[:, 1:2], in0=wil, in1=cl, op=ALU.mult), reads=[('wim', b), ck], writes=['tn1'])
                c.op('dve', lambda e: e.scalar_tensor_tensor(out=xe[0][:, s:s + 1], in0=wrl, scalar=cl, in1=tn[:, 0:1], op0=ALU.mult, op1=ALU.subtract),
                     reads=[('wre', b), ck, 'tn0'], writes=[('xe', 0, s)])
                c.op('dve', lambda e: e.scalar_tensor_tensor(out=xe[1][:, s:s + 1], in0=wrl, scalar=sl_, in1=tn[:, 1:2], op0=ALU.mult, op1=ALU.add),
                     reads=[('wre', b), sk, 'tn1'], writes=[('xe', 1, s)])
                c.op('dve', lambda e: e.tensor_tensor(out=M[0][:], in0=wim[b][:], in1=sn_, op=ALU.mult), reads=[('wim', b), sk], writes=[MK[0]])
                c.op('dve', lambda e: e.tensor_tensor(out=M[1][:], in0=wre[b][:], in1=cs_, op=ALU.mult), reads=[('wre', b), ck], writes=[MK[1]])
                c.op('dve', lambda e: e.tensor_tensor(out=xrb[b][:], in0=M[1][:], in1=M[0][:], op=ALU.subtract), reads=[MK[0], MK[1]], writes=[('xrb', b)])
                c.op('dve', lambda e: e.tensor_tensor(out=M[2][:], in0=wre[b][:], in1=sn_, op=ALU.mult), reads=[('wre', b), sk], writes=[MK[2]])
                c.op('dve', lambda e: e.tensor_tensor(out=M[3][:], in0=wim[b][:], in1=cs_, op=ALU.mult), reads=[('wim', b), ck], writes=[MK[3]])
                c.op('dve', lambda e: e.scalar_tensor_tensor(out=xib[b][:], in0=M[2][:], scalar=-1.0, in1=M[3][:], op0=ALU.mult, op1=ALU.subtract),
                     reads=[MK[2], MK[3]], writes=[('xib', b)])

            def stageC(ic, s):
                kc, b = s // 4, s % 2
                c.op('pe', lambda e: e.matmul(PS[4 + kc][:, 0:TS], cT[0][:, s, :], xrb[b][:], start=(s % 4 == 0), stop=False),
                     reads=['cT', ('xrb', b)], writes=[('ps', 4 + kc)], pe_acc=True)
                c.op('pe', lambda e: e.matmul(PS[4 + kc][:, 0:TS], cT[1][:, s, :], xib[b][:], start=False, stop=(s % 4 == 3)),
                     reads=['cT', ('xib', b)], writes=[('ps', 4 + kc)], pe_acc=True)

            seq = [(ic, s) for ic in range(nchunk) for s in range(16)]
            stageA(*seq[0])
            stageBpre(*seq[0])
            stageBrim(*seq[0])
            for idx, (ic, s) in enumerate(seq):
                u, uk = ub[ic % 2], ('ub', ic % 2)
                if s == 0 and ic + 1 < nchunk:
                    c.dma('sp', ub[(ic + 1) % 2][:], uv[:, :, (ic + 1) * TS:(ic + 2) * TS], writes=[('ub', (ic + 1) % 2)])
                if idx + 1 < len(seq):
                    stageA(*seq[idx + 1])
                    stageBpre(*seq[idx + 1])
                stageB(ic, s)
                if idx + 1 < len(seq):
                    stageBrim(*seq[idx + 1])
                stageC(ic, s)
                if s != 15:
                    continue
                for kc in range(4):
                    c.op('dve', lambda e, kc=kc: e.scalar_tensor_tensor(out=yv[:, kc, :], in0=u[:, kc, :], scalar=dsk[:, 0, kc:kc + 1], in1=PS[4 + kc][:, 0:TS],
                                                                         op0=ALU.mult, op1=ALU.add), reads=[uk, 'dsk', ('ps', 4 + kc)], writes=['yv'])
                c.op('pool', lambda e: e.tensor_tensor(out=yt[:], in0=yv[:], in1=yv[:], op=ALU.mult), reads=['yv'], writes=['yt'])
                c.op('pool', lambda e: e.tensor_scalar(out=yt[:], in0=yt[:], scalar1=0.044715, scalar2=1.0, op0=ALU.mult, op1=ALU.add), reads=['yt'], writes=['yt'])
                c.op('pool', lambda e: e.tensor_tensor(out=yt[:], in0=yt[:], in1=yv[:], op=ALU.mult), reads=['yt', 'yv'], writes=['yt'])
                c.op('act', lambda e: e.activation(out=yt[:], in_=yt[:], func=AF.Sigmoid, scale=1.5957691216), reads=['yt'], writes=['yt'])
                c.op('dve', lambda e: e.tensor_tensor(out=yv[:], in0=yv[:], in1=yt[:], op=ALU.mult), reads=['yt', 'yv'], writes=['yv'])
                c.op('act', lambda e: e.activation(out=ygb[:], in_=yv[:], func=AF.Copy), reads=['yv'], writes=['ygb'])
                o, ok = osb[ic % 2], ('osb', ic % 2)
                for mo in range(4):
                    pg = mo % 4
                    for kc in range(4):
                        c.op('pe', lambda e, kc=kc, mo=mo: e.matmul(PS[pg][:, 0:TS], wg[:, kc, mo * 128:(mo + 1) * 128], ygb[:, kc, :], start=(kc == 0), stop=(kc == 3)),
                             reads=["wglusb", "ygb"], writes=[('ps', pg)], pe_acc=True)
                    c.op('act', lambda e, mo=mo: e.activation(out=sg[:], in_=PS[pg][:, 0:TS], func=AF.Sigmoid, bias=dsk[:, 1, mo:mo + 1]),
                         reads=[('ps', pg), 'dsk'], writes=['sg'])
                    c.op('dve', lambda e, mo=mo: e.tensor_tensor(out=o[:, mo, :], in0=yv[:, mo, :], in1=sg[:], op=ALU.mult), reads=['yv', 'sg'], writes=[ok])
                c.dma('sp', ov[:, :, ic * TS:(ic + 1) * TS], o[:], reads=[ok])
        c.barrier()
        if stop_after <= 3:
            c.es.close()
            return nc
        with ExitStack() as es:
            NCP = NCC * 128
            kslT = sb(es, "kslT", [128, L], BF16)
            kwT = sb(es, "kwT", [128, L], BF16)
            vslm = sb(es, "vslm", [128, NB, 2, 65], BF16)
            vwm = sb(es, "vwm", [128, NB, 2, 65], BF16)
            kccT = sb(es, "kccT", [128, NCP], BF16)
            vcc = sb(es, "vcc", [128, NCC, 2, 65], BF16)
            wm = sb(es, "wm", [128, 6, 512], BF16)
            visb = sb(es, "visb", [128, 8, 128], BF16)
            vfb = sb(es, "vfb", [128, 2, 256], F32)
            ovb = sb(es, "ovb", [128, NCC, 128], BF16)
            indb = sb(es, "indb", [128, L], BF16)
            c.dma('sp', kslT[:], ksl_s[:, :], writes=['kslT'])
            c.dma('sp', kwT[:], kw_s[:, :], writes=['kwT'])
            c.op('dve', lambda e: e.memset(vslm[:], 1.0), writes=['vslm'])
            c.op('dve', lambda e: e.memset(vwm[:], 1.0), writes=['vwm'])
            c.op('dve', lambda e: e.memset(vcc[:], 1.0), writes=['vcc'])
            for g in range(2):
                for c0 in range(0, NB, 8):
                    c1 = min(NB, c0 + 8)
                    c.dma('sp', vslm[:, c0:c1, g, 0:64], vsl_t[c0 * 128:c1 * 128, g * 64:(g + 1) * 64].rearrange("(c p) d -> p c d", p=128), reads=['vslm'], writes=['vslm'])
                    c.dma('sp', vwm[:, c0:c1, g, 0:64], vw_t[c0 * 128:c1 * 128, g * 64:(g + 1) * 64].rearrange("(c p) d -> p c d", p=128), reads=['vwm'], writes=['vwm'])
            c.dma('pool', wm[:], wmask[:, :, :], writes=['wm'])
            c.dma('pool', visb[:], vis8[:, :, :], writes=['visb'])
            c.dma('sp', vfb[:], vfrel[:, :, :], writes=['vfb'])
            c.dma('pool', ovb[:], ovl.rearrange("c p s -> p c s"), writes=['ovb'])
            c.dma('pool', indb[:], indneg[:, :], writes=['indb'])
            with ExitStack() as e2:
                kvT = [sb(e2, "kvT%d" % i, [128, L], BF16) for i in range(2)]
                w1c = [sb(e2, "w1c%d" % i, [128, 32, 256], BF16) for i in range(2)]
                w2p = sb(e2, "w2p", [128, 2, 2, 2, 128], BF16)
                w2v = sb(e2, "w2v", [128, 2, 64], BF16)
                pes = sb(e2, "pes", [64, 2, 32], BF16)
                bia = sb(e2, "bia", [128, 2, 2], F32)
                hid = sb(e2, "hid", [128, 2, 2, 2, NCP], BF16)
                hb = sb(e2, "hb", [128, NCP], F32)
                ht = sb(e2, "ht", [128, NCP], F32)
                c.dma('sp', kvT[0][:], kc_s[:, :], writes=[('kvT', 0)])
                c.dma('sp', kvT[1][:], vc_s[:, :], writes=[('kvT', 1)])
                c.op('pool', lambda e: e.memset(hid[:], 0.0), writes=['hid'])
                for kv in range(2):
                    for half in range(2):
                        c.dma('pool', w1c[kv][half * 64:(half + 1) * 64, :, :], cw1[kv].rearrange("(j d) h -> d j h", d=64), writes=[('w1c', kv)])
                    c.dma('pool', pes[:, kv, :], peT[kv], writes=['pes'])
                    for g in range(2):
                        for hf in range(2):
                            c.dma('pool', w2p[:, kv, g, hf, :], cw2p[kv, g, hf * 128:(hf + 1) * 128, :], writes=['w2p'])
                for hf in range(2):
                    c.dma('pool', w2v[:, hf, :], cw2[1, hf * 128:(hf + 1) * 128, :], writes=['w2v'])
                for kv in range(2):
                    for hf in range(2):
                        for j in range(32):
                            c.op('pe', lambda e, j=j: e.matmul(PS[0][:, 0:1], w1c[kv][0:64, j, hf * 128:(hf + 1) * 128], pes[:, kv, j:j + 1], start=(j == 0), stop=(j == 31)),
                                 reads=[('w1c', kv), 'pes'], writes=[('ps', 0)], pe_acc=True)
                        c.op('act', lambda e: e.activation(out=bia[:, kv, hf:hf + 1], in_=PS[0][:, 0:1], func=AF.Copy), reads=[('ps', 0)], writes=['bia'])
                for kv in range(2):
                    for g in range(2):
                        for hf in range(2):
                            pi = 1 + (g * 2 + hf) % 2
                            for j in range(32):
                                c.op('pe', lambda e, j=j: e.matmul(PS[pi][:, 0:NCMP], w1c[kv][g * 64:(g + 1) * 64, j, hf * 128:(hf + 1) * 128],
                                                                   kvT[kv][g * 64:(g + 1) * 64, j:j + 16 * (NCMP - 1) + 1:16], start=(j == 0), stop=(j == 31)),
                                     reads=[('w1c', kv), ('kvT', kv)], writes=[('ps', pi)], pe_acc=True)
                            c.op('act', lambda e: e.activation(out=hb[:, 0:NCMP], in_=PS[pi][:, 0:NCMP], func=AF.Identity, bias=bia[:, kv, hf:hf + 1]),
                                 reads=[('ps', pi), 'bia'], writes=['hb'])
                            c.op('dve', lambda e: e.tensor_tensor(out=ht[:, 0:NCMP], in0=hb[:, 0:NCMP], in1=hb[:, 0:NCMP], op=ALU.mult), reads=['hb'], writes=['ht'])
                            c.op('dve', lambda e: e.tensor_scalar(out=ht[:, 0:NCMP], in0=ht[:, 0:NCMP], scalar1=0.044715, scalar2=1.0, op0=ALU.mult, op1=ALU.add), reads=['ht'], writes=['ht'])
                            c.op('dve', lambda e: e.tensor_tensor(out=ht[:, 0:NCMP], in0=ht[:, 0:NCMP], in1=hb[:, 0:NCMP], op=ALU.mult), reads=['ht', 'hb'], writes=['ht'])
                            c.op('act', lambda e: e.activation(out=ht[:, 0:NCMP], in_=ht[:, 0:NCMP], func=AF.Sigmoid, scale=1.5957691216), reads=['ht'], writes=['ht'])
                            c.op('dve', lambda e: e.tensor_tensor(out=hid[:, kv, g, hf, 0:NCMP], in0=hb[:, 0:NCMP], in1=ht[:, 0:NCMP], op=ALU.mult), reads=['ht', 'hb', 'hid'], writes=['hid'])
                n = 0
                for g in range(2):
                    for hf in range(2):
                        c.op('pe', lambda e, n=n: e.matmul(PS[3][:, 0:NCP], w2p[:, 0, g, hf, :], hid[:, 0, g, hf, :], start=(n == 0), stop=(n == 3)),
                             reads=['w2p', 'hid'], writes=[('ps', 3)], pe_acc=True)
                        n += 1
                c.op('act', lambda e: e.activation(out=kccT[:], in_=PS[3][:, 0:NCP], func=AF.Copy), reads=[('ps', 3)], writes=['kccT'])
                for cc in range(NCC):
                    for g in range(2):
                        for hf in range(2):
                            c.op('pe', lambda e: e.matmul(PS[4][:, 0:64], hid[:, 1, g, hf, cc * 128:(cc + 1) * 128], w2v[:, hf, :], start=(hf == 0), stop=(hf == 1)),
                                 reads=['w2v', 'hid'], writes=[('ps', 4)], pe_acc=True)
                        c.op('act', lambda e: e.activation(out=vcc[:, cc, g, 0:64], in_=PS[4][:, 0:64], func=AF.Copy), reads=[('ps', 4), 'vcc'], writes=['vcc'])
                c.barrier()
            qe = [sb(es, "qe%d" % i, [128, 512], BF16) for i in range(2)]
            qo = [sb(es, "qo%d" % i, [128, 512], BF16) for i in range(2)]
            qp = [[sb(es, "qp%d_%d" % (i, g), [128, 512], BF16) for g in range(2)] for i in range(2)]
            for i in range(2):
                for g in range(2):
                    c.op('dve', lambda e, i=i, g=g: e.memset(qp[i][g][:], 0.0), writes=[('qp', i, g)])
            ge = [sb(es, "ge0", [128, 24, 128], F32)] * 2
            go = [sb(es, "go0", [128, 24, 128], F32)] * 2
            gt = [sb(es, "gt%d" % i, [128, 24, 128], F32) for i in range(2)]
            Eb = [sb(es, "Eb%d" % i, [128, 512], BF16) for i in range(3)]
            rows_ = [[sb(es, "row%d_%d" % (x, i), [128, 2, 512], F32) for i in range(2)] for x in range(3)]
            dq = []

            def popdq():
                if dq:
                    f = dq.pop(0)
                    if f is not None:
                        f()

            def drain():
                while dq:
                    popdq()
            Bc = sb(es, "Bc", [128, 512], F32)
            Bc2 = sb(es, "Bc2", [64, 512], F32)
            tmpi = sb(es, "tmpi", [128, 512], F32)
            impT = sb(es, "impT", [128, 128], F32)
            sc = sb(es, "sc", [128, 128], F32)
            sc2 = sb(es, "sc2", [128, 128], F32)
            m8 = sb(es, "m8", [128, 16], F32)
            nsl = sb(es, "nsl", [128, 128], F32)
            nsT = sb(es, "nsT", [128, 4, 128], BF16)
            acc = sb(es, "acc", [64, 512], F32)
            acct = sb(es, "acct", [64, 512], F32)
            accb = [sb(es, "accb%d" % i, [64, 512], BF16) for i in range(2)]
            ecnt = [0]
            scnt = [0]
            onv = onsa_s.rearrange("(h d) n -> d h n", d=64)

            pend = [None]

            def flush():
                if pend[0] is not None:
                    f = pend[0]
                    pend[0] = None
                    f()

            def score_chunk(KT, g, kc0, Qg, qk, extra, Vm, vkey, g_, po, first, last):
                si = scnt[0] % 2
                scnt[0] += 1
                ne = len(extra)
                c.op('pe', lambda e: e.matmul(PS[si][:, :], KT[:, kc0 * 128:(kc0 + 1) * 128], Qg, start=True, stop=(ne == 0)),
                     reads=[qk, 'kslT', 'kwT', 'kccT'], writes=[('ps', si)], pe_acc=True)
                for ix, (lt, rh, rk) in enumerate(extra):
                    c.op('pe', lambda e, lt=lt, rh=rh, ix=ix: e.matmul(PS[si][:, :], lt, rh, start=False, stop=(ix == ne - 1)),
                         reads=rk, writes=[('ps', si)], pe_acc=True)
                ei = ecnt[0] % 3
                ecnt[0] += 1
                E, ek = Eb[ei], ('E', ei)
                c.op('act', lambda e: e.activation(out=E[:], in_=PS[si][:, :], func=AF.Exp, scale=0.125), reads=[('ps', si)], writes=[ek])
                flush()
                popdq()
                return E, ek

            def fin1(po, x, g, gk_i, rw, rk):
                c.op('dve', lambda e: e.tensor_scalar(out=rw[64:65, 0, :], in0=PS[po][64:65, :], scalar1=1e-30, scalar2=None, op0=ALU.max),
                     reads=[('ps', po)], writes=[rk + '0'])
                c.op('dve', lambda e: e.reciprocal(out=rw[64:65, 0, :], in_=rw[64:65, 0, :]), reads=[rk + '0'], writes=[rk + '0'])
                gsl = gt[gk_i][64:65, g * 12 + x:g * 12 + x + 10:3, :]
                c.op('dve', lambda e: e.tensor_tensor(out=rw[64:65, 1, :].rearrange("p (h q) -> p h q", h=4), in0=rw[64:65, 0, :].rearrange("p (h q) -> p h q", h=4),
                                                      in1=gsl, op=ALU.mult), reads=[rk + '0', ('gt', gk_i)], writes=[rk + '1'])

            def fin2(po, first, rw, rk):
                c.op('pe', lambda e: e.matmul(PS[6][0:64, :], ones_f[64:65, 0:64], rw[64:65, 1, :], start=True, stop=True), reads=[rk + '1', 'ones_f'], writes=[('ps', 6)])
                c.op('act', lambda e: e.activation(out=Bc2[:], in_=PS[6][0:64, :], func=AF.Copy), reads=[('ps', 6)], writes=['Bc2'])
                if first:
                    c.op('dve', lambda e: e.tensor_tensor(out=acc[:], in0=PS[po][0:64, :], in1=Bc2[:], op=ALU.mult), reads=[('ps', po), 'Bc2'], writes=['acc'])
                else:
                    c.op('dve', lambda e: e.tensor_tensor(out=acct[:], in0=PS[po][0:64, :], in1=Bc2[:], op=ALU.mult), reads=[('ps', po), 'Bc2'], writes=['acct'])
                    c.op('pool', lambda e: e.tensor_tensor(out=acc[:], in0=acc[:], in1=acct[:], op=ALU.add), reads=['acc', 'acct'], writes=['acc'])

            for k in range(NBo):
                bi = k % 2
                qk, gk = ('qb', bi), ('gt', bi)
                c.dma('sp', qe[bi][:].rearrange("p (h q) -> p h q", h=4), q_s[:, 2 * k, :, :], writes=[('qe', bi)])
                c.dma('sp', qo[bi][:].rearrange("p (h q) -> p h q", h=4), q_s[:, 2 * k + 1, :, :], writes=[('qo', bi)])
                for g in range(2):
                    blend('dve', qp[bi][g][g * 64:(g + 1) * 64, :], qe[bi][g * 64:(g + 1) * 64, :], qo[bi][g * 64:(g + 1) * 64, :],
                          [('qe', bi), ('qo', bi)], [('qp', bi, g)], g * 64, (g + 1) * 64)
                c.dma('sp', ge[bi][64:65, :, :], g_s[:, 2 * k * 128:(2 * k + 1) * 128].rearrange("(o r) q -> o r q", o=1), writes=[('ge', 0)])
                c.dma('sp', go[bi][64:65, :, :], g_s[:, (2 * k + 1) * 128:(2 * k + 2) * 128].rearrange("(o r) q -> o r q", o=1), writes=[('go', 0)])
                blend('pool', gt[bi][64:65, :, :], ge[bi][64:65, :, :], go[bi][64:65, :, :], [('ge', 0), ('go', 0)], [gk], 64, 65)
                ccd = k // 8
                for g in range(2):
                    Qg = qp[bi][g][:]
                    qk = ('qp', bi, g)
                    for cc in range(ccd + 1):
                        E, ek = score_chunk(kccT, g, cc, Qg, qk, [], None, None, None, 2, cc == 0, cc == ccd)
                        if cc == ccd:
                            for h in range(4):
                                c.op('pool' if h % 2 else 'dve', lambda e, h=h: e.tensor_tensor(out=E[:, h * 128:(h + 1) * 128], in0=E[:, h * 128:(h + 1) * 128],
                                                                                                 in1=visb[:, k % 8, :], op=ALU.mult), reads=[ek, 'visb'], writes=[ek])
                        def pv_cmp(cc=cc, E=E, ek=ek, g=g, ccd=ccd):
                            c.op('pe', lambda e: e.matmul(PS[2][0:65, :], vcc[:, cc, g, :], E[:], start=(cc == 0), stop=(cc == ccd)),
                                 reads=[ek, 'vcc'], writes=[('ps', 2)], pe_acc=True)
                            c.op('pe', lambda e: e.matmul(PS[5][:, :], ovb[:, cc, :], E[:], start=(cc == 0), stop=(cc == ccd)),
                                 reads=[ek, 'ovb'], writes=[('ps', 5)], pe_acc=True)
                        pend[0] = pv_cmp
                    flush()
                    par_ = (k * 2 + g) % 2
                    rc_, rw_, rs_ = rows_[0][par_], rows_[1][par_], rows_[2][par_]
                    kc_, kw_, ks_ = 'rowc%d' % par_, 'roww%d' % par_, 'rows%d' % par_
                    fin1(2, 0, g, bi, rc_, kc_)

                    def stepB(rc_=rc_, kc_=kc_):
                        c.op('pe', lambda e: e.matmul(PS[6][:, :], ones_f[64:65, :], rc_[64:65, 0, :], start=True, stop=True), reads=[kc_ + '0', 'ones_f'], writes=[('ps', 6)])
                        c.op('act', lambda e: e.activation(out=Bc[:], in_=PS[6][:, :], func=AF.Copy), reads=[('ps', 6)], writes=['Bc'])
                        c.op('dve', lambda e: e.tensor_tensor(out=tmpi[:], in0=PS[5][:, :], in1=Bc[:], op=ALU.mult), reads=[('ps', 5), 'Bc'], writes=['tmpi'])
                        c.op('dve', lambda e: e.tensor_reduce(out=impT[:], in_=tmpi[:].rearrange("p (h q) -> p q h", h=4), axis=AX.X, op=ALU.add), reads=['tmpi'], writes=['impT'])

                    def stepS1(k=k):
                        c.op('pe', lambda e: e.transpose(PS[7][:, 0:128], impT[:], id_f[:]), reads=['impT', 'id_f'], writes=[('ps', 7)])
                        vo = 128 - 4 * k
                        c.op('dve', lambda e: e.tensor_tensor(out=sc[:], in0=PS[7][:, 0:128], in1=vfb[:, 0, vo:vo + 128], op=ALU.mult), reads=[('ps', 7), 'vfb'], writes=['sc'])
                        c.op('dve', lambda e: e.tensor_tensor(out=sc[:], in0=sc[:], in1=vfb[:, 1, vo:vo + 128], op=ALU.add), reads=['sc', 'vfb'], writes=['sc'])
                        c.op('dve', lambda e: e.memset(sc[:, 0:1], BIG), reads=['sc'], writes=['sc'])
                        c.op('dve', lambda e: e.max(out=m8[:, 0:8], in_=sc[:]), reads=['sc'], writes=['m8'])
                        c.op('dve', lambda e: e.match_replace(out=sc2[:], in_to_replace=m8[:, 0:8], in_values=sc[:], imm_value=-3.0e38), reads=['sc', 'm8'], writes=['sc2'])
                        c.op('dve', lambda e: e.max(out=m8[:, 8:16], in_=sc2[:]), reads=['sc2', 'm8'], writes=['m8'])
                        c.op('dve', lambda e: e.tensor_scalar(out=nsl[:], in0=sc[:], scalar1=m8[:, 15:16], scalar2=None, op0=ALU.is_lt), reads=['sc', 'm8'], writes=['nsl'])

                    def stepS2():
                        c.op('pe', lambda e: e.transpose(PS[7][:, 128:256], nsl[:], id_f[:]), reads=['nsl', 'id_f'], writes=[('ps', 7)])
                        for h in range(4):
                            c.op('act' if h % 2 else 'dve', lambda e, h=h: (e.activation(out=nsT[:, h, :], in_=PS[7][:, 128:256], func=AF.Copy) if h % 2
                                                                            else e.tensor_copy(out=nsT[:, h, :], in_=PS[7][:, 128:256])),
                                 reads=[('ps', 7), 'nsT'], writes=['nsT'])

                    dq.extend([stepB, lambda rc_=rc_, kc_=kc_: fin2(2, True, rc_, kc_), stepS1, None, None, stepS2])
                    wl = [wi for wi in range(6) if 2 * k - 4 + wi >= 0]
                    for wi in wl:
                        kcn = 2 * k - 4 + wi
                        E, ek = score_chunk(kwT, g, kcn, Qg, qk, [(id_bf[:], wm[:, wi, :], ['id_bf', 'wm'])], None, None, None, 3, False, False)

                        def pv_win(kcn=kcn, wi=wi, E=E, ek=ek, g=g, wl=wl):
                            c.op('pe', lambda e: e.matmul(PS[3][0:65, :], vwm[:, kcn, g, :], E[:], start=(wi == wl[0]), stop=(wi == wl[-1])),
                                 reads=[ek, 'vwm'], writes=[('ps', 3)], pe_acc=True)
                        pend[0] = pv_win
                    flush()
                    fin1(3, 2, g, bi, rw_, kw_)
                    drain()
                    dq.append(lambda rw_=rw_, kw_=kw_: fin2(3, False, rw_, kw_))
                    nk = 2 * k + 2
                    for kcn in range(nk):
                        extra = [(indb[:, kcn * 128:(kcn + 1) * 128], nsT[:].rearrange("p h q -> p (h q)"), ['indb', 'nsT'])]
                        if kcn >= 2 * k:
                            extra.append((id_bf[:], wm[:, 4 + kcn - 2 * k, :], ['id_bf', 'wm']))
                        E, ek = score_chunk(kslT, g, kcn, Qg, qk, extra, None, None, None, 4, False, False)

                        def pv_slc(kcn=kcn, E=E, ek=ek, g=g, nk=nk):
                            c.op('pe', lambda e: e.matmul(PS[4][0:65, :], vslm[:, kcn, g, :], E[:], start=(kcn == 0), stop=(kcn == nk - 1)),
                                 reads=[ek, 'vslm'], writes=[('ps', 4)], pe_acc=True)
                        pend[0] = pv_slc
                    flush()
                    drain()
                    fin1(4, 1, g, bi, rs_, ks_)

                    def stepE(rs_=rs_, ks_=ks_, g=g, k=k):
                        fin2(4, False, rs_, ks_)
                        ab, abk = accb[g], ('accb', g)
                        c.op('act', lambda e: e.activation(out=ab[:], in_=acc[:], func=AF.Copy), reads=['acc'], writes=[abk])
                        c.dma('sp', onv[:, g * 4:(g + 1) * 4, k * 128:(k + 1) * 128], ab[:].rearrange("p (h q) -> p h q", h=4), reads=[abk])
                    dq.append(stepE)
            drain()
        c.barrier()

        if stop_after <= 4:
            c.es.close()
            return nc
        with ExitStack() as es:
            T = 512
            wo = load_w(es, "wo", wout, 8, D)
            hx = sb(es, "hx", [128, 8, T], F32)
            he = sb(es, "he", [128, 8, T], F32)
            ho = sb(es, "ho", [128, 8, T], F32)
            oo = sb(es, "oo", [128, 8, T], BF16)
            oe_ = sb(es, "oe_", [128, 4, T], BF16)
            od_ = sb(es, "od_", [128, 4, T], BF16)
            h1b = ch(h1_s).rearrange("p c (b j q) -> p c b j q", j=2, q=128)
            o5b = ch(os5_s).rearrange("p c (b j q) -> p c b j q", j=2, q=128)
            onb = ch(onsa_s)
            h2v = ch(h2_s)
            v4 = lambda t: t[:].rearrange("p c (b q) -> p c b q", q=128)
            for it in range(Lo // T):
                for kc in range(8):
                    c.dma('sp', v4(he)[:, kc], h1b[:, kc, 4 * it:4 * it + 4, 0, :], writes=['he'])
                    c.dma('sp', v4(ho)[:, kc], h1b[:, kc, 4 * it:4 * it + 4, 1, :], writes=['ho'])
                for kc in range(4):
                    c.dma('sp', v4(oe_)[:, kc], o5b[:, kc, 4 * it:4 * it + 4, 0, :], writes=['oe_'])
                    c.dma('sp', v4(od_)[:, kc], o5b[:, kc, 4 * it:4 * it + 4, 1, :], writes=['od_'])
                c.dma('sp', oo[:, 0:4, :], onb[:, :, it * T:(it + 1) * T], writes=['oo_n'])
                blend('dve', hx[:], he[:], ho[:], ['he', 'ho'], ['hx'])
                blend('pool', oo[:, 4:8, :], oe_[:], od_[:], ['oe_', 'od_'], ['oo_s'])
                for mo in range(8):
                    po = mo % 2
                    for kc in range(8):
                        c.op('pe', lambda e, kc=kc, mo=mo: e.matmul(PS[po][:, :], wo[:, kc, mo * 128:(mo + 1) * 128], oo[:, kc, :], start=(kc == 0), stop=(kc == 7)),
                             reads=['wo', 'oo_n', 'oo_s'], writes=[('ps', po)], pe_acc=True)
                    c.op('dve', lambda e, mo=mo: e.tensor_tensor(out=hx[:, mo, :], in0=PS[po][:, :], in1=hx[:, mo, :], op=ALU.add), reads=[('ps', po), 'hx'], writes=['hx'])
                c.dma('sp', h2v[:, :, it * T:(it + 1) * T], hx[:], reads=['hx'])
        c.barrier()

        ffn_phase(h2_s, h3_s, Lo, f2w1, f2w3, f2w2, 2, "B")

        with ExitStack() as es:
            T = 512
            wgt = load_w(es, "wgt", wgate, 8, D)
            wpl = load_w(es, "wpl", wple, 2, D)
            xt = sb(es, "cxt", [128, 8, T], F32)
            sq = sb(es, "csq", [128, 8, T], BF16)
            a = sb(es, "ca", [128, 8, T], BF16)
            rstd = sb(es, "crs", [128, T], F32)
            pf = sb(es, "cpf", [128, 2, T], F32)
            pb = sb(es, "cpb", [128, 2, T], BF16)
            gt_ = sb(es, "cgt", [128, T], F32)
            tt = sb(es, "ctt", [128, T], F32)
            of = sb(es, "cof", [128, 8, T], F32)
            h3v, pv, ov_ = ch(h3_s), ch(pT), ch(outT)
            for it in range(Lo // T):
                c.dma('sp', xt[:], h3v[:, :, it * T:(it + 1) * T], writes=['xt'])
                c.dma('sp', pf[:], pv[:, :, it * T:(it + 1) * T], writes=['pf'])
                c.op('act', lambda e: e.activation(out=pb[:], in_=pf[:], func=AF.Copy), reads=['pf'], writes=['pb'])
                rmsnorm((sq, rstd), xt[:], 'xt', 3, T, a, 'a', 6)
                for mo in range(8):
                    pg, pp = mo % 2, 2 + mo % 2
                    for kc in range(8):
                        c.op('pe', lambda e, kc=kc, mo=mo: e.matmul(PS[pg][:, :], wgt[:, kc, mo * 128:(mo + 1) * 128], a[:, kc, :], start=(kc == 0), stop=(kc == 7)),
                             reads=['wgt', 'a'], writes=[('ps', pg)], pe_acc=True)
                    for kc in range(2):
                        c.op('pe', lambda e, kc=kc, mo=mo: e.matmul(PS[pp][:, :], wpl[:, kc, mo * 128:(mo + 1) * 128], pb[:, kc, :], start=(kc == 0), stop=(kc == 1)),
                             reads=['wpl', 'pb'], writes=[('ps', pp)], pe_acc=True)
                    c.op('act', lambda e: e.activation(out=gt_[:], in_=PS[pg][:, :], func=AF.Sigmoid), reads=[('ps', pg)], writes=['gt_'])
                    c.op('dve', lambda e: e.tensor_tensor(out=tt[:], in0=PS[pp][:, :], in1=gt_[:], op=ALU.mult), reads=[('ps', pp), 'gt_'], writes=['tt'])
                    c.op('pool', lambda e, mo=mo: e.tensor_tensor(out=xt[:, mo, :], in0=xt[:, mo, :], in1=tt[:], op=ALU.add), reads=['tt', 'xt'], writes=['xt'])
                rmsnorm((sq, rstd), xt[:], 'xt', 4, T, of, 'of', 6)
                c.dma('sp', ov_[:, :, it * T:(it + 1) * T], of[:], reads=['of'])
        c.barrier()
    c.es.close()
    return nc


def _host_consts(L, j):
    NCMP = L // 16 - 1
    NCC = max(1, L // 2048)
    pI = np.arange(128)[:, None]
    iI = np.arange(128)[None, :]
    tri = np.where(pI > iI, NEGM, 0.0).astype(np.float32)
    part = np.where(pI <= iI, NEGM, 0.0).astype(np.float32)
    full = np.zeros((128, 128), np.float32)
    none = np.full((128, 128), NEGM, np.float32)
    lst = [part, full, full, full, tri, none] if j == 0 else [none, part, full, full, full, tri]
    wmask = np.stack([np.tile(m, (1, 4)) for m in lst], axis=1).astype(np.float32)
    kk = np.arange(8)[None, :, None]
    vis8 = ((16 * pI[:, :, None] + 31) <= (256 * kk + 128 * j + iI[:, None, :])).astype(np.float32)
    q = np.arange(128)[:, None]
    sp = np.arange(256)[None, :] - 128
    cur = 2 * j + (q >= 64)
    valid = sp <= cur
    forced = (sp == cur) | (sp == cur - 1)
    V = (valid & ~forced).astype(np.float32)
    Fm = np.where(valid & forced, BIG, np.where(valid, 0.0, -BIG)).astype(np.float32)
    vfrel = np.stack([V, Fm], axis=1).astype(np.float32)
    m = (np.arange(NCC * 128)).reshape(NCC, 128, 1)
    s = np.arange(128)[None, None, :]
    ovl = ((16 * m < 64 * s + 64) & (16 * m + 32 > 64 * s) & (m < NCMP)).astype(np.float32)
    key = np.arange(L)[None, :]
    indneg = np.where(key // 64 == np.arange(128)[:, None], NEGM, 0.0).astype(np.float32)
    par = np.zeros((128, 2), np.float32)
    par[:, 0] = j
    par[:, 1] = 1 - j
    return dict(wmask=wmask, vis8=vis8, vfrel=vfrel, ovl=ovl, indneg=indneg, par=par)


def _host_weights(I, L):
    f = lambda a: np.ascontiguousarray(np.asarray(a, dtype=np.float32))
    gl = lambda v: f(np.asarray(v).reshape(8, 128).T)
    gains = np.stack([gl(I["norm_ffn1"][0]), gl(I["norm_mix"][0]), gl(I["norm_ffn2"][0]), gl(I["norm_ple"][0]), gl(I["norm_final"])])
    w_in = np.asarray(I["w_in"][0])
    cols = []
    for h in range(4):
        for g in range(2):
            cols += list(range((g * 4 + h) * 64, (g * 4 + h) * 64 + 64))
    cols += list(range(512, 640)) + list(range(768, 896)) + list(range(1024, 1152))
    cols += list(range(640, 768)) + list(range(896, 1024)) + list(range(1152, 1280))
    cols += list(range(1304, 1816)) + list(range(1280, 1304))
    cols = np.array(cols)
    winp = f(w_in[:, cols])
    c0 = cols[:896]
    sw = (c0 // 64) * 64 + (c0 % 64 + 32) % 64
    wins = f(w_in[:, sw])
    p = np.arange(128)
    d = p % 64
    inv = (np.float32(10000.0) ** (-(d % 32).astype(np.float32) / np.float32(32))).astype(np.float32)
    pos = np.arange(L, dtype=np.float32)
    ang = (pos[None, :] * inv[:, None]).astype(np.float32)
    ropec = np.cos(ang).astype(np.float32)
    ropes = (np.sin(ang) * np.where(d < 32, -1.0, 1.0)[:, None]).astype(np.float32)
    peT = np.stack([f(np.asarray(I["cmp_pe_k"][0]).T), f(np.asarray(I["cmp_pe_v"][0]).T)])
    cw1 = np.stack([f(I["cmp_wk1"][0]), f(I["cmp_wv1"][0])])
    cw2 = np.stack([f(I["cmp_wk2"][0]), f(I["cmp_wv2"][0])])
    cw2p = np.zeros((2, 2, 256, 128), np.float32)
    for kv in range(2):
        for g in range(2):
            cw2p[kv, g, :, g * 64:(g + 1) * 64] = cw2[kv]
    are, aim, ldt = np.asarray(I["s5_a_re"][0]), np.asarray(I["s5_a_im"][0]), np.asarray(I["s5_log_dt"][0])
    s5p = np.zeros((3, 128, 16), np.float32)
    s5B = np.zeros((2, 16, 128, 128), np.float32)
    s5C = np.zeros((2, 16, 128, 128), np.float32)
    Bre, Bim = np.asarray(I["s5_b_re"][0]), np.asarray(I["s5_b_im"][0])
    Cre, Cim = np.asarray(I["s5_c_re"][0]), np.asarray(I["s5_c_im"][0])
    for s in range(16):
        for hh in range(2):
            g = 2 * s + hh
            s5p[0, hh * 64:(hh + 1) * 64, s] = are[g]
            s5p[1, hh * 64:(hh + 1) * 64, s] = aim[g]
            s5p[2, hh * 64:(hh + 1) * 64, s] = ldt[g]
            c0_ = (g % 8) * 16
            s5B[0, s, hh * 64:(hh + 1) * 64, c0_:c0_ + 16] = Bre[g]
            s5B[1, s, hh * 64:(hh + 1) * 64, c0_:c0_ + 16] = Bim[g]
            s5C[0, s, hh * 64:(hh + 1) * 64, c0_:c0_ + 16] = Cre[g].T
            s5C[1, s, hh * 64:(hh + 1) * 64, c0_:c0_ + 16] = Cim[g].T
    s5d = np.stack([f(np.asarray(I["s5_d"][0]).reshape(4, 128).T), f(np.asarray(I["s5_b_glu"][0]).reshape(4, 128).T)])
    return dict(gains=f(gains), f1w1=f(I["ffn1_w1"][0]), f1w3=f(I["ffn1_w3"][0]), f1w2=f(I["ffn1_w2"][0]),
                f2w1=f(I["ffn2_w1"][0]), f2w3=f(I["ffn2_w3"][0]), f2w2=f(I["ffn2_w2"][0]),
                winp=winp, wins=wins, ropec=ropec, ropes=ropes, peT=f(peT), cw1=f(cw1), cw2=f(cw2), cw2p=cw2p,
                s5p=s5p, s5B=s5B, s5C=s5C, s5d=f(s5d), wglu=f(I["s5_w_glu"][0]),
                wout=f(I["w_out"][0]), wgate=f(I["w_ple_gate"][0]), wple=f(I["w_ple"][0]))


def make_in_maps(I):
    x = np.asarray(I["x"]); p = np.asarray(I["p"])
    B, L = x.shape[0], x.shape[1]
    W = _host_weights(I, L)
    maps = []
    for b in range(B):
        xT = np.ascontiguousarray(x[b].T)
        pb = p[0, b].reshape(L // 256, 2, 128, 256)
        for j in range(2):
            m = dict(W)
            m.update(_host_consts(L, j))
            m["xT"] = xT
            m["pT"] = np.ascontiguousarray(pb[:, j].reshape(L // 2, 256).T)
            maps.append(m)
    return maps, B, L


def assemble(results, B, L):
    out = np.zeros((B, L, D), np.float32)
    ov = out.reshape(B, L // 256, 2, 128, D)
    for b in range(B):
        for j in range(2):
            o = np.asarray(results[b * 2 + j]["outT"], dtype=np.float32)
            ov[b, :, j] = o.T.reshape(L // 256, 128, D)
    return out


def kernel(**inputs):
    maps, B, L = make_in_maps(inputs)
    nc = build(L)
    res = run_bass_kernel_spmd(nc, maps, core_ids=list(range(len(maps))))
    return assemble(res.results, B, L)
```

```python
from contextlib import ExitStack
import math
import numpy as np
import concourse.bass as bass
import concourse.mybir as mybir
from concourse.bass_utils import run_bass_kernel_spmd

F32 = mybir.dt.float32
BF16 = mybir.dt.bfloat16
I32 = mybir.dt.int32
AF = mybir.ActivationFunctionType
ALU = mybir.AluOpType
AX = mybir.AxisListType

D = 1024
DFF = 2816
NFF = DFF // 128
EPS = 1e-6
NEGM = -30000.0
BIG = 1.0e9
SEM_ROT = 8000
DMA_ROT = 500


class Ctx:
    def __init__(self, nc):
        self.nc = nc
        self.es = ExitStack()
        self.eng = {'pe': nc.tensor, 'act': nc.scalar, 'dve': nc.vector, 'pool': nc.gpsimd, 'sp': nc.sync}
        self.nsem = 0
        self.csem = {}
        self.cseq = {}
        for e in ('pe', 'act', 'dve', 'pool'):
            self.csem[e] = self.new_sem()
            self.cseq[e] = 0
        self.dsl = {q: [[self.new_sem(), 0] for _ in range(4)] for q in ('sp', 'pool', 'act')}
        self.dnext = {q: 0 for q in self.dsl}
        self.waited = {e: {} for e in self.eng}
        self.bufs = {}
        self.latest = {}

    def new_sem(self):
        self.nsem += 1
        return self.es.enter_context(self.nc.semaphore("s%d" % self.nsem))

    def _wait(self, e, ev):
        if ev is None:
            return
        sem, val = ev
        w = self.waited[e]
        if w.get(sem, 0) >= val:
            return
        self.eng[e].wait_ge(sem, val)
        w[sem] = val

    def _deps(self, e, reads, writes, pe_acc):
        evs = []
        for k in reads:
            b = self.bufs.get(k)
            if b is not None and b[0]:
                evs.extend(b[0])
        for k in writes:
            b = self.bufs.get(k)
            if b is None:
                continue
            if b[0]:
                if e in self.dsl and b[2] in self.dsl and not b[1]:
                    pass
                elif not (pe_acc and b[2] == 'pe' and not b[1]):
                    evs.extend(b[0])
            evs.extend(b[1])
        for ev in evs:
            self._wait(e, ev)

    def _record(self, e, ev, reads, writes):
        self.latest[ev[0]] = ev[1]
        for k in reads:
            b = self.bufs.setdefault(k, [[], [], None])
            b[1].append(ev)
        for k in writes:
            b = self.bufs.get(k)
            if b is not None and e in self.dsl and b[2] in self.dsl and not b[1] and k not in reads:
                b[0].append(ev)
            else:
                self.bufs[k] = [[ev], [], e]

    def op(self, e, fn, reads=(), writes=(), pe_acc=False):
        self._deps(e, reads, writes, pe_acc)
        ins = fn(self.eng[e])
        if self.cseq[e] >= SEM_ROT:
            self.csem[e] = self.new_sem()
            self.cseq[e] = 0
        self.cseq[e] += 1
        ins.then_inc(self.csem[e], 1)
        self._record(e, (self.csem[e], self.cseq[e]), reads, writes)

    def dma(self, q, out, in_, reads=(), writes=()):
        sl = self.dsl[q]
        i = self.dnext[q]
        self.dnext[q] = (i + 1) % len(sl)
        if sl[i][1] >= DMA_ROT:
            self._wait(q, (sl[i][0], 16 * sl[i][1]))
            sl[i] = [self.new_sem(), 0]
        sem, cnt = sl[i]
        if cnt > 0:
            self._wait(q, (sem, 16 * cnt))
        self._deps(q, reads, writes, False)
        ins = self.eng[q].dma_start(out=out, in_=in_)
        ins.then_inc(sem, 16)
        sl[i][1] = cnt + 1
        self._record(q, (sem, 16 * (cnt + 1)), reads, writes)

    def barrier(self):
        for e in self.eng:
            for sem, val in list(self.latest.items()):
                self._wait(e, (sem, val))
        self.bufs = {}


def build(L, stop_after=99, debug=False):
    NB = L // 128
    NBo = NB // 2
    Lo = L // 2
    NCMP = L // 16 - 1
    NCC = max(1, L // 2048)
    nc = bass.Bass("TRN2", target_bir_lowering=False)
    c = Ctx(nc)

    def din(name, shape, dt=F32):
        return nc.dram_tensor(name, list(shape), dt, kind="ExternalInput").ap()

    def dscr(name, shape, dt):
        return nc.dram_tensor(name, list(shape), dt, kind=("ExternalOutput" if debug else "Internal")).ap()

    par = din("par", [128, 2])
    xT = din("xT", [D, L])
    pT = din("pT", [256, Lo])
    gains = din("gains", [5, 128, 8])
    f1w1 = din("f1w1", [D, DFF]); f1w3 = din("f1w3", [D, DFF]); f1w2 = din("f1w2", [DFF, D])
    f2w1 = din("f2w1", [D, DFF]); f2w3 = din("f2w3", [D, DFF]); f2w2 = din("f2w2", [DFF, D])
    winp = din("winp", [D, 1816]); wins = din("wins", [D, 896])
    ropec = din("ropec", [128, L]); ropes = din("ropes", [128, L])
    peT = din("peT", [2, 64, 32])
    cw1 = din("cw1", [2, 2048, 256]); cw2p = din("cw2p", [2, 2, 256, 128])
    cw2 = din("cw2", [2, 256, 64])
    s5p = din("s5p", [3, 128, 16])
    s5B = din("s5B", [2, 16, 128, 128])
    s5C = din("s5C", [2, 16, 128, 128])
    s5d = din("s5d", [2, 128, 4])
    wglu = din("wglu", [512, 512])
    wout = din("wout", [D, D]); wgate = din("wgate", [D, D]); wple = din("wple", [256, D])
    wmask = din("wmask", [128, 6, 512])
    vis8 = din("vis8", [128, 8, 128])
    vfrel = din("vfrel", [128, 2, 256])
    ovl = din("ovl", [NCC, 128, 128])
    indneg = din("indneg", [128, L])
    outT = nc.dram_tensor("outT", [D, Lo], F32, kind="ExternalOutput").ap()

    h1_s = dscr("h1_s", [D, L], F32)
    h2_s = dscr("h2_s", [D, Lo], F32)
    h3_s = dscr("h3_s", [D, Lo], F32)
    q_s = dscr("q_s", [128, NB, 4, 128], BF16)
    kc_s = dscr("kc_s", [128, L], BF16); vc_s = dscr("vc_s", [128, L], BF16)
    ksl_s = dscr("ksl_s", [128, L], BF16); kw_s = dscr("kw_s", [128, L], BF16)
    vsl_t = dscr("vsl_t", [L, 128], BF16); vw_t = dscr("vw_t", [L, 128], BF16)
    g_s = dscr("g_s", [24, L], F32)
    u_s = dscr("u_s", [512, L], BF16)
    os5_s = dscr("os5_s", [512, L], BF16)
    onsa_s = dscr("onsa_s", [512, Lo], BF16)

    def ch(ap):
        return ap.rearrange("(c p) n -> p c n", p=128)

    with ExitStack() as g0:
        sb = lambda es, name, shape, dt: es.enter_context(nc.sbuf_tensor(name, list(shape), dt))
        PS = [g0.enter_context(nc.psum_tensor("ps%d" % i, [128, 512], F32)) for i in range(8)]
        ones_bf = sb(g0, "ones_bf", [128, 128], BF16)
        ones_f = sb(g0, "ones_f", [128, 128], F32)
        id_f = sb(g0, "id_f", [128, 128], F32)
        id_bf = sb(g0, "id_bf", [128, 128], BF16)
        gn = sb(g0, "gn", [128, 5, 8], F32)
        c.op('dve', lambda e: e.memset(ones_bf[:], 1.0), writes=['ones_bf'])
        c.op('dve', lambda e: e.memset(ones_f[:], 1.0), writes=['ones_f'])
        c.op('pool', lambda e: e.affine_select(out=id_f[:], in_=ones_f[:], pattern=[[-1, 128]], compare_op=ALU.is_equal,
                                               fill=0.0, base=0, channel_multiplier=1), reads=['ones_f'], writes=['id_f'])
        c.op('dve', lambda e: e.tensor_copy(out=id_bf[:], in_=id_f[:]), reads=['id_f'], writes=['id_bf'])
        parv = sb(g0, "parv", [128, 2], F32)
        for i in range(5):
            c.dma('sp', gn[:, i, :], gains[i], writes=['gn'])
        c.dma('sp', parv[:], par[:, :], writes=['parv'])

        def blend(eng, out, ev, od, rd, wr, p0=0, p1=128):
            eng = 'dve'
            c.op(eng, lambda e: e.tensor_scalar(out=out, in0=od, scalar1=parv[p0:p1, 0:1], scalar2=None, op0=ALU.mult),
                 reads=list(rd) + ['parv'], writes=wr)
            c.op(eng, lambda e: e.scalar_tensor_tensor(out=out, in0=ev, scalar=parv[p0:p1, 1:2], in1=out, op0=ALU.mult, op1=ALU.add),
                 reads=list(rd) + ['parv'] + list(wr), writes=wr)

        def rmsnorm(es_tiles, xt, xk, gi, T, a_out, a_key, ps_i):
            sq, rstd = es_tiles
            c.op('act', lambda e: e.activation(out=sq[:, 0:8, 0:T], in_=xt, func=AF.Square), reads=[xk], writes=['sq'])
            for kc in range(8):
                c.op('pe', lambda e, kc=kc: e.matmul(PS[ps_i][:, 0:T], ones_bf[:], sq[:, kc, 0:T], start=(kc == 0), stop=(kc == 7)),
                     reads=['sq', 'ones_bf'], writes=[('ps', ps_i)], pe_acc=True)
            c.op('act', lambda e: e.activation(out=rstd[:, 0:T], in_=PS[ps_i][:, 0:T], func=AF.Sqrt, bias=EPS, scale=1.0 / D),
                 reads=[('ps', ps_i)], writes=['rstd'])
            c.op('dve', lambda e: e.reciprocal(out=rstd[:, 0:T], in_=rstd[:, 0:T]), reads=['rstd'], writes=['rstd'])
            for kc in range(8):
                c.op('dve', lambda e, kc=kc: e.scalar_tensor_tensor(out=a_out[:, kc, 0:T], in0=xt[:, kc, :], scalar=gn[:, gi, kc:kc + 1],
                                                                     in1=rstd[:, 0:T], op0=ALU.mult, op1=ALU.mult),
                     reads=[xk, 'gn', 'rstd'], writes=[a_key])

        def load_w(es, name, src, kchunks, ncols):
            t = sb(es, name, [128, kchunks, ncols], BF16)
            for kc in range(kchunks):
                c.dma('pool', t[:, kc, :], src[kc * 128:(kc + 1) * 128, :], writes=[name])
            return t

        def ffn_phase(src, dst, ntok, w1d, w3d, w2d, gi, tag):
            T = 512
            with ExitStack() as es:
                w1 = load_w(es, "w1" + tag, w1d, 8, DFF)
                w3 = load_w(es, "w3" + tag, w3d, 8, DFF)
                w2 = load_w(es, "w2" + tag, w2d, NFF, D)
                xts = [sb(es, "xt%d%s" % (i, tag), [128, 8, T], F32) for i in range(2)]
                sq = sb(es, "sq" + tag, [128, NFF, T], BF16)
                a = sb(es, "a" + tag, [128, 8, T], BF16)
                rstd = sb(es, "rstd" + tag, [128, T], F32)
                s1 = [sb(es, "s1%d%s" % (i, tag), [128, T], F32) for i in range(2)]
                srcv, dstv = ch(src), ch(dst)
                nt = ntok // T
                c.dma('sp', xts[0][:], srcv[:, :, 0:T], writes=[('xt', 0)])
                for it in range(nt):
                    xt, xk = xts[it % 2], ('xt', it % 2)
                    if it + 1 < nt:
                        c.dma('sp', xts[(it + 1) % 2][:], srcv[:, :, (it + 1) * T:(it + 2) * T], writes=[('xt', (it + 1) % 2)])
                    rmsnorm((sq, rstd), xt[:], xk, gi, T, a, 'a', 6)
                    for m in range(NFF):
                        p1, p3 = m % 2, 2 + m % 2
                        for kc in range(8):
                            c.op('pe', lambda e, kc=kc, m=m, p1=p1: e.matmul(PS[p1][:, 0:T], w1[:, kc, m * 128:(m + 1) * 128], a[:, kc, :],
                                                                               start=(kc == 0), stop=(kc == 7)),
                                 reads=['a', 'w1' + tag], writes=[('ps', p1)], pe_acc=True)
                        for kc in range(8):
                            c.op('pe', lambda e, kc=kc, m=m, p3=p3: e.matmul(PS[p3][:, 0:T], w3[:, kc, m * 128:(m + 1) * 128], a[:, kc, :],
                                                                               start=(kc == 0), stop=(kc == 7)),
                                 reads=['a', 'w3' + tag], writes=[('ps', p3)], pe_acc=True)
                        c.op('act', lambda e, m=m, p1=p1: e.activation(out=s1[m % 2][:], in_=PS[p1][:, 0:T], func=AF.Silu),
                             reads=[('ps', p1)], writes=[('s1', m % 2)])
                        c.op('dve', lambda e, m=m, p3=p3: e.tensor_tensor(out=sq[:, m, :], in0=PS[p3][:, 0:T], in1=s1[m % 2][:], op=ALU.mult),
                             reads=[('ps', p3), ('s1', m % 2)], writes=[('g', m)])
                    for mo in range(8):
                        po = 4 + mo % 2
                        for m in range(NFF):
                            c.op('pe', lambda e, m=m, mo=mo, po=po: e.matmul(PS[po][:, 0:T], w2[:, m, mo * 128:(mo + 1) * 128], sq[:, m, :],
                                                                               start=(m == 0), stop=(m == NFF - 1)),
                                 reads=[('g', m), 'sq', 'w2' + tag], writes=[('ps', po)], pe_acc=True)
                        c.op('dve', lambda e, mo=mo, po=po: e.scalar_tensor_tensor(out=xt[:, mo, :], in0=PS[po][:, 0:T], scalar=0.5, in1=xt[:, mo, :],
                                                                                     op0=ALU.mult, op1=ALU.add),
                             reads=[('ps', po), xk], writes=[xk])
                    c.dma('act', dstv[:, :, it * T:(it + 1) * T], xt[:], reads=[xk])
            c.barrier()

        ffn_phase(xT, h1_s, L, f1w1, f1w3, f1w2, 0, "A")
        if stop_after <= 1:
            c.es.close()
            return nc

        with ExitStack() as es:
            T = 512
            wp = load_w(es, "wp", winp, 8, 1816)
            ws = load_w(es, "ws", wins, 8, 896)
            xts = [sb(es, "p2x%d" % i, [128, 8, T], F32) for i in range(2)]
            sq = sb(es, "p2sq", [128, 8, T], BF16)
            a = sb(es, "p2a", [128, 8, T], BF16)
            rstd = sb(es, "p2r", [128, T], F32)
            rc = [sb(es, "p2c%d" % i, [128, T], F32) for i in range(2)]
            rs = [sb(es, "p2s%d" % i, [128, T], F32) for i in range(2)]
            t1 = [sb(es, "p2t1%d" % i, [128, T], F32) for i in range(2)]
            t2 = [sb(es, "p2t2%d" % i, [128, T], F32) for i in range(2)]
            ob = [sb(es, "p2o%d" % i, [128, T], BF16) for i in range(3)]
            gb = [sb(es, "p2g%d" % i, [24, T], F32) for i in range(2)]
            h1v = ch(h1_s)
            nt = L // T
            cnt = [0, 0]
            c.dma('sp', xts[0][:], h1v[:, :, 0:T], writes=[('xt', 0)])
            for it in range(nt):
                xt, xk = xts[it % 2], ('xt', it % 2)
                t0 = it * T
                if it + 1 < nt:
                    c.dma('sp', xts[(it + 1) % 2][:], h1v[:, :, (it + 1) * T:(it + 2) * T], writes=[('xt', (it + 1) % 2)])
                ri = it % 2
                c.dma('sp', rc[ri][:], ropec[:, t0:t0 + T], writes=[('rc', ri)])
                c.dma('sp', rs[ri][:], ropes[:, t0:t0 + T], writes=[('rs', ri)])
                rmsnorm((sq, rstd), xt[:], xk, 1, T, a, 'a', 6)

                def proj(pcols, rope, dst_ap, ob_view=None):
                    i = cnt[0] % 2
                    cnt[0] += 1
                    pz, pzs = i, 2 + i
                    for kc in range(8):
                        c.op('pe', lambda e, kc=kc: e.matmul(PS[pz][:, 0:T], wp[:, kc, pcols:pcols + 128], a[:, kc, :],
                                                             start=(kc == 0), stop=(kc == 7)),
                             reads=['a', 'wp'], writes=[('ps', pz)], pe_acc=True)
                    oi = cnt[1] % 3
                    cnt[1] += 1
                    o, ok = ob[oi], ('ob', oi)
                    if rope:
                        for kc in range(8):
                            c.op('pe', lambda e, kc=kc: e.matmul(PS[pzs][:, 0:T], ws[:, kc, pcols:pcols + 128], a[:, kc, :],
                                                                 start=(kc == 0), stop=(kc == 7)),
                                 reads=['a', 'ws'], writes=[('ps', pzs)], pe_acc=True)
                        c.op('dve', lambda e: e.tensor_tensor(out=t1[i][:], in0=PS[pz][:, 0:T], in1=rc[ri][:], op=ALU.mult),
                             reads=[('ps', pz), ('rc', ri)], writes=[('t1', i)])
                        c.op('dve', lambda e: e.tensor_tensor(out=t2[i][:], in0=PS[pzs][:, 0:T], in1=rs[ri][:], op=ALU.mult),
                             reads=[('ps', pzs), ('rs', ri)], writes=[('t2', i)])
                        c.op('pool', lambda e: e.tensor_tensor(out=o[:], in0=t1[i][:], in1=t2[i][:], op=ALU.add),
                             reads=[('t1', i), ('t2', i)], writes=[ok])
                    else:
                        c.op('act', lambda e: e.activation(out=o[:], in_=PS[pz][:, 0:T], func=AF.Copy), reads=[('ps', pz)], writes=[ok])
                    src = o[:] if ob_view is None else ob_view(o)
                    c.dma('sp', dst_ap, src, reads=[ok])

                proj(512, True, kc_s[:, t0:t0 + T])
                proj(640, True, ksl_s[:, t0:t0 + T])
                proj(768, True, kw_s[:, t0:t0 + T])
                proj(896, False, vc_s[:, t0:t0 + T])
                for uc in range(4):
                    proj(1280 + uc * 128, False, u_s[uc * 128:(uc + 1) * 128, t0:t0 + T])
                for h in range(4):
                    proj(h * 128, True, q_s[:, it * 4:(it + 1) * 4, h, :], ob_view=lambda o: o[:].rearrange("p (b q) -> p b q", b=4))
                i = cnt[0] % 2
                cnt[0] += 1
                pz = i
                for kc in range(8):
                    c.op('pe', lambda e, kc=kc: e.matmul(PS[pz][0:24, 0:T], wp[:, kc, 1792:1816], a[:, kc, :], start=(kc == 0), stop=(kc == 7)),
                         reads=['a', 'wp'], writes=[('ps', pz)], pe_acc=True)
                c.op('act', lambda e: e.activation(out=gb[ri][:], in_=PS[pz][0:24, 0:T], func=AF.Sigmoid), reads=[('ps', pz)], writes=[('gb', ri)])
                c.dma('sp', g_s[:, t0:t0 + T], gb[ri][:], reads=[('gb', ri)])
                for tb in range(4):
                    i = cnt[0] % 2
                    cnt[0] += 1
                    pz = i
                    for kc in range(8):
                        c.op('pe', lambda e, kc=kc: e.matmul(PS[pz][:, 0:256], a[:, kc, tb * 128:(tb + 1) * 128], wp[:, kc, 1024:1280],
                                                             start=(kc == 0), stop=(kc == 7)),
                             reads=['a', 'wp'], writes=[('ps', pz)], pe_acc=True)
                    oi = cnt[1] % 3
                    cnt[1] += 1
                    o, ok = ob[oi], ('ob', oi)
                    c.op('act', lambda e: e.activation(out=o[:, 0:256], in_=PS[pz][:, 0:256], func=AF.Copy), reads=[('ps', pz)], writes=[ok])
                    r0 = t0 + tb * 128
                    c.dma('sp', vsl_t[r0:r0 + 128, :], o[:, 0:128], reads=[ok])
                    c.dma('sp', vw_t[r0:r0 + 128, :], o[:, 128:256], reads=[ok])
        c.barrier()

        if stop_after <= 2:
            c.es.close()
            return nc
        with ExitStack() as es:
            TS = 512
            NL = TS.bit_length() - 1
            prm = sb(es, "s5prm", [128, 3, 16], F32)
            wk = {n: sb(es, "s5k_" + n, [128, 16], F32) for n in
                  ("dt", "rre", "th", "rmag", "y", "yf", "f", "adj", "ang", "sn", "cs", "abr", "abi", "den", "t1", "t2", "bcr", "bci")}
            yi = sb(es, "s5yi", [128, 16], I32)
            cosT = sb(es, "s5cos", [128, 16, TS], F32)
            sinT = sb(es, "s5sin", [128, 16, TS], F32)
            tmpd = [sb(es, "s5tmp%d" % i, [128, TS], F32) for i in range(2)]
            blt = [sb(es, "s5bl%d" % i, [128, 128], F32) for i in range(4)]
            bT = [sb(es, "s5bT%d" % i, [128, 16, 128], BF16) for i in range(2)]
            cT = [sb(es, "s5cT%d" % i, [128, 16, 128], BF16) for i in range(2)]
            dsk = sb(es, "s5dsk", [128, 2, 4], F32)
            wg = load_w(es, "wglusb", wglu, 4, 512)
            xe = [sb(es, "s5xe%d" % i, [128, 16], F32) for i in range(2)]
            for i in range(3):
                c.dma('sp', prm[:, i, :], s5p[i], writes=['prm'])
            for i in range(2):
                c.dma('sp', dsk[:, i, :], s5d[i], writes=['dsk'])
                c.op('dve', lambda e, i=i: e.memset(xe[i][:], 0.0), writes=[('xe', i, s) for s in range(16)])
                for s in range(16):
                    c.dma('pool', cT[i][:, s, :], s5C[i, s], writes=['cT'])
            P = lambda n: wk[n][:]

            def dv(fn, rd, wr):
                c.op('dve', fn, reads=['prm'] + rd, writes=wr)
            c.op('act', lambda e: e.activation(out=P("dt"), in_=prm[:, 2, :], func=AF.Exp), reads=['prm'], writes=['dt'])
            dv(lambda e: e.tensor_tensor(out=P("rre"), in0=prm[:, 0, :], in1=P("dt"), op=ALU.mult), ['dt'], ['rre'])
            dv(lambda e: e.tensor_tensor(out=P("th"), in0=prm[:, 1, :], in1=P("dt"), op=ALU.mult), ['dt'], ['th'])
            c.op('act', lambda e: e.activation(out=P("rmag"), in_=P("rre"), func=AF.Exp), reads=['rre'], writes=['rmag'])
            dv(lambda e: e.tensor_scalar(out=P("y"), in0=P("th"), scalar1=1.0 / (2 * math.pi), scalar2=None, op0=ALU.mult), ['th'], ['y'])
            dv(lambda e: e.tensor_copy(out=yi[:], in_=P("y")), ['y'], ['yi'])
            dv(lambda e: e.tensor_copy(out=P("yf"), in_=yi[:]), ['yi'], ['yf'])
            dv(lambda e: e.tensor_tensor(out=P("f"), in0=P("y"), in1=P("yf"), op=ALU.subtract), ['y', 'yf'], ['f'])
            dv(lambda e: e.tensor_scalar(out=P("adj"), in0=P("f"), scalar1=0.5, scalar2=None, op0=ALU.is_gt), ['f'], ['adj'])
            dv(lambda e: e.tensor_tensor(out=P("f"), in0=P("f"), in1=P("adj"), op=ALU.subtract), ['f', 'adj'], ['f'])
            dv(lambda e: e.tensor_scalar(out=P("adj"), in0=P("f"), scalar1=-0.5, scalar2=None, op0=ALU.is_lt), ['f'], ['adj'])
            dv(lambda e: e.tensor_tensor(out=P("f"), in0=P("f"), in1=P("adj"), op=ALU.add), ['f', 'adj'], ['f'])
            dv(lambda e: e.tensor_scalar(out=P("ang"), in0=P("f"), scalar1=6.283185, scalar2=None, op0=ALU.mult), ['f'], ['ang'])
            c.op('act', lambda e: e.activation(out=P("sn"), in_=P("ang"), func=AF.Sin), reads=['ang'], writes=['sn'])
            dv(lambda e: e.tensor_scalar(out=P("t1"), in0=P("ang"), scalar1=-1.0, scalar2=None, op0=ALU.mult), ['ang'], ['t1'])
            dv(lambda e: e.tensor_tensor(out=P("t1"), in0=P("t1"), in1=P("ang"), op=ALU.max), ['ang', 't1'], ['t1'])
            dv(lambda e: e.tensor_scalar(out=P("t1"), in0=P("t1"), scalar1=-1.0, scalar2=1.5707963, op0=ALU.mult, op1=ALU.add), ['t1'], ['t1'])
            c.op('act', lambda e: e.activation(out=P("cs"), in_=P("t1"), func=AF.Sin), reads=['t1'], writes=['cs'])
            dv(lambda e: e.tensor_tensor(out=P("abr"), in0=P("rmag"), in1=P("cs"), op=ALU.mult), ['rmag', 'cs'], ['abr'])
            dv(lambda e: e.tensor_tensor(out=P("abi"), in0=P("rmag"), in1=P("sn"), op=ALU.mult), ['rmag', 'sn'], ['abi'])
            dv(lambda e: e.tensor_scalar(out=P("abr"), in0=P("abr"), scalar1=-1.0, scalar2=None, op0=ALU.add), ['abr'], ['abr'])
            dv(lambda e: e.tensor_tensor(out=P("den"), in0=prm[:, 0, :], in1=prm[:, 0, :], op=ALU.mult), [], ['den'])
            dv(lambda e: e.tensor_tensor(out=P("t1"), in0=prm[:, 1, :], in1=prm[:, 1, :], op=ALU.mult), ['t1'], ['t1'])
            dv(lambda e: e.tensor_tensor(out=P("den"), in0=P("den"), in1=P("t1"), op=ALU.add), ['den', 't1'], ['den'])
            dv(lambda e: e.reciprocal(out=P("den"), in_=P("den")), ['den'], ['den'])
            dv(lambda e: e.tensor_tensor(out=P("t1"), in0=P("abr"), in1=prm[:, 0, :], op=ALU.mult), ['abr', 't1'], ['t1'])
            dv(lambda e: e.tensor_tensor(out=P("t2"), in0=P("abi"), in1=prm[:, 1, :], op=ALU.mult), ['abi'], ['t2'])
            dv(lambda e: e.tensor_tensor(out=P("t1"), in0=P("t1"), in1=P("t2"), op=ALU.add), ['t1', 't2'], ['t1'])
            dv(lambda e: e.tensor_tensor(out=P("bcr"), in0=P("t1"), in1=P("den"), op=ALU.mult), ['t1', 'den'], ['bcr'])
            dv(lambda e: e.tensor_tensor(out=P("t1"), in0=P("abi"), in1=prm[:, 0, :], op=ALU.mult), ['abi', 't1'], ['t1'])
            dv(lambda e: e.tensor_tensor(out=P("t2"), in0=P("abr"), in1=prm[:, 1, :], op=ALU.mult), ['abr', 't2'], ['t2'])
            dv(lambda e: e.tensor_tensor(out=P("t1"), in0=P("t1"), in1=P("t2"), op=ALU.subtract), ['t1', 't2'], ['t1'])
            dv(lambda e: e.tensor_tensor(out=P("bci"), in0=P("t1"), in1=P("den"), op=ALU.mult), ['t1', 'den'], ['bci'])
            for s in range(16):
                en = 'dve'
                tm = tmpd[s % 2]
                ck, sk, tk = ('cos', s), ('sin', s), ('tmpd', s % 2)
                c.op(en, lambda e, s=s: e.tensor_copy(out=cosT[:, s, 0:1], in_=wk["cs"][:, s:s + 1]), reads=['cs'], writes=[ck])
                c.op(en, lambda e, s=s: e.tensor_copy(out=sinT[:, s, 0:1], in_=wk["sn"][:, s:s + 1]), reads=['sn'], writes=[sk])
                m = 1
                while m < TS:
                    cr, sr = cosT[:, s, m - 1:m], sinT[:, s, m - 1:m]
                    c.op(en, lambda e, s=s, m=m, sr=sr: e.tensor_scalar(out=tm[:, 0:m], in0=sinT[:, s, 0:m], scalar1=sr, scalar2=None, op0=ALU.mult),
                         reads=[sk], writes=[tk])
                    c.op(en, lambda e, s=s, m=m, sr=sr: e.tensor_scalar(out=tm[:, m:2 * m], in0=cosT[:, s, 0:m], scalar1=sr, scalar2=None, op0=ALU.mult),
                         reads=[ck, sk], writes=[tk])
                    c.op(en, lambda e, s=s, m=m, cr=cr: e.scalar_tensor_tensor(out=cosT[:, s, m:2 * m], in0=cosT[:, s, 0:m], scalar=cr, in1=tm[:, 0:m],
                                                                               op0=ALU.mult, op1=ALU.subtract), reads=[ck, tk], writes=[ck])
                    c.op(en, lambda e, s=s, m=m, cr=cr: e.scalar_tensor_tensor(out=sinT[:, s, m:2 * m], in0=sinT[:, s, 0:m], scalar=cr, in1=tm[:, m:2 * m],
                                                                               op0=ALU.mult, op1=ALU.add), reads=[ck, sk, tk], writes=[sk])
                    m *= 2
            for s in range(16):
                bre, bim, bpr, bpi = blt
                c.dma('sp', bre[:], s5B[0, s], writes=['bre'])
                c.dma('sp', bim[:], s5B[1, s], writes=['bim'])
                bcr, bci = wk["bcr"][:, s:s + 1], wk["bci"][:, s:s + 1]
                c.op('dve', lambda e: e.tensor_scalar(out=bpr[:], in0=bim[:], scalar1=bci, scalar2=None, op0=ALU.mult), reads=['bim', 'bci'], writes=['bpr'])
                c.op('dve', lambda e: e.scalar_tensor_tensor(out=bpr[:], in0=bre[:], scalar=bcr, in1=bpr[:], op0=ALU.mult, op1=ALU.subtract),
                     reads=['bre', 'bcr', 'bpr'], writes=['bpr'])
                c.op('dve', lambda e: e.tensor_scalar(out=bpi[:], in0=bre[:], scalar1=bci, scalar2=None, op0=ALU.mult), reads=['bre', 'bci'], writes=['bpi'])
                c.op('dve', lambda e: e.scalar_tensor_tensor(out=bpi[:], in0=bim[:], scalar=bcr, in1=bpi[:], op0=ALU.mult, op1=ALU.add),
                     reads=['bim', 'bcr', 'bpi'], writes=['bpi'])
                for i, (bp, bk) in enumerate(((bpr, 'bpr'), (bpi, 'bpi'))):
                    c.op('pe', lambda e, bp=bp: e.transpose(PS[i][:, 0:128], bp[:], id_f[:]), reads=[bk, 'id_f'], writes=[('ps', i)])
                    c.op('act', lambda e, i=i, s=s: e.activation(out=bT[i][:, s, :], in_=PS[i][:, 0:128], func=AF.Copy), reads=[('ps', i)], writes=['bT'])

            ub = [sb(es, "s5u%d" % i, [128, 4, TS], BF16) for i in range(2)]
            bur = [sb(es, "s5bur%d" % i, [128, TS], F32) for i in range(2)]
            bui = [sb(es, "s5bui%d" % i, [128, TS], F32) for i in range(2)]
            mm = [sb(es, "s5m%d" % i, [128, TS], F32) for i in range(8)]
            mp = [sb(es, "s5mp%d" % i, [128, TS], F32) for i in range(8)]
            rr = [sb(es, "s5rr%d" % i, [128, TS], F32) for i in range(2)]
            rim = [sb(es, "s5ri%d" % i, [128, TS], F32) for i in range(2)]
            wre = [sb(es, "s5wr%d" % i, [128, TS], F32) for i in range(2)]
            wim = [sb(es, "s5wi%d" % i, [128, TS], F32) for i in range(2)]
            xrb = [sb(es, "s5xr%d" % i, [128, TS], BF16) for i in range(2)]
            xib = [sb(es, "s5xi%d" % i, [128, TS], BF16) for i in range(2)]
            tn = sb(es, "s5tn", [128, 4], F32)
            yv = sb(es, "s5yv", [128, 4, TS], F32)
            yt = sb(es, "s5yt", [128, 4, TS], F32)
            ygb = sb(es, "s5ygb", [128, 4, TS], BF16)
            sg = sb(es, "s5sg", [128, TS], F32)
            osb = [sb(es, "s5o%d" % i, [128, 4, TS], BF16) for i in range(2)]
            uv = ch(u_s)
            ov = ch(os5_s)
            nchunk = L // TS
            c.dma('sp', ub[0][:], uv[:, :, 0:TS], writes=[('ub', 0)])

            def stageA(ic, s):
                u, uk = ub[ic % 2], ('ub', ic % 2)
                kc, b = s // 4, s % 2
                c.op('pe', lambda e: e.matmul(PS[b][:, 0:TS], bT[0][:, s, :], u[:, kc, :], start=True, stop=True), reads=['bT', uk], writes=[('ps', b)])
                c.op('pe', lambda e: e.matmul(PS[2 + b][:, 0:TS], bT[1][:, s, :], u[:, kc, :], start=True, stop=True), reads=['bT', uk], writes=[('ps', 2 + b)])
                c.op('act', lambda e: e.activation(out=bur[b][:], in_=PS[b][:, 0:TS], func=AF.Copy), reads=[('ps', b)], writes=[('bur', b)])
                c.op('act', lambda e: e.activation(out=bui[b][:], in_=PS[2 + b][:, 0:TS], func=AF.Copy), reads=[('ps', 2 + b)], writes=[('bui', b)])

            def stageBpre(ic, s):
                kc, b = s // 4, s % 2
                ck, sk = ('cos', s), ('sin', s)
                cs_, sn_ = cosT[:, s, :], sinT[:, s, :]
                M = [mm[4 * b + i] for i in range(4)]
                MK = [('mm', 4 * b + i) for i in range(4)]
                c.op('pool', lambda e: e.tensor_tensor(out=M[0][:], in0=bur[b][:], in1=cs_, op=ALU.mult), reads=[('bur', b), ck], writes=[MK[0]])
                c.op('pool', lambda e: e.tensor_tensor(out=M[1][:], in0=bui[b][:], in1=sn_, op=ALU.mult), reads=[('bui', b), sk], writes=[MK[1]])
                c.op('pool', lambda e: e.tensor_tensor(out=rr[b][:], in0=M[0][:], in1=M[1][:], op=ALU.add), reads=[MK[0], MK[1]], writes=[('rr', b)])
                c.op('pool', lambda e: e.tensor_tensor(out=M[2][:], in0=bui[b][:], in1=cs_, op=ALU.mult), reads=[('bui', b), ck], writes=[MK[2]])
                c.op('pool', lambda e: e.tensor_tensor(out=M[3][:], in0=bur[b][:], in1=sn_, op=ALU.mult), reads=[('bur', b), sk], writes=[MK[3]])
                c.op('pool', lambda e: e.tensor_tensor(out=rim[b][:], in0=M[2][:], in1=M[3][:], op=ALU.subtract), reads=[MK[2], MK[3]], writes=[('ri', b)])

            def stageB(ic, s):
                kc, b = s // 4, s % 2
                ck, sk = ('cos', s), ('sin', s)
                cs_, sn_ = cosT[:, s, :], sinT[:, s, :]
                M = [mp[4 * b + i] for i in range(4)]
                MK = [('mp', 4 * b + i) for i in range(4)]
                rb = wk["rmag"][:, s:s + 1].to_broadcast([128, TS])
                c.op('dve', lambda e: e.tensor_tensor_scan(out=wre[b][:], data0=rb, data1=rr[b][:], initial=xe[0][:, s:s + 1], op0=ALU.mult, op1=ALU.add),
                     reads=['rmag', ('rr', b), ('xe', 0, s)], writes=[('wre', b)])
                c.op('dve', lambda e: e.tensor_tensor_scan(out=wim[b][:], data0=rb, data1=rim[b][:], initial=xe[1][:, s:s + 1], op0=ALU.mult, op1=ALU.add),
                     reads=['rmag', ('ri', b), ('xe', 1, s)], writes=[('wim', b)])
                cl, sl_ = cosT[:, s, TS - 1:TS], sinT[:, s, TS - 1:TS]
                wrl, wil = wre[b][:, TS - 1:TS], wim[b][:, TS - 1:TS]
                c.op('dve', lambda e: e.tensor_tensor(out=tn[:, 0:1], in0=wil, in1=sl_, op=ALU.mult), reads=[('wim', b), sk], writes=['tn0'])
                c.op('dve', lambda e: e.tensor_tensor(out=tn[:, 1:2], in0=wil, in1=cl, op=ALU.mult), reads=[('wim', b), ck], writes=['tn1'])
                c.op('dve', lambda e: e.scalar_tensor_tensor(out=xe[0][:, s:s + 1], in0=wrl, scalar=cl, in1=tn[:, 0:1], op0=ALU.mult, op1=ALU.subtract),
                     reads=[('wre', b), ck, 'tn0'], writes=[('xe', 0, s)])
                c.op('dve', lambda e: e.scalar_tensor_tensor(out=xe[1][:, s:s + 1], in0=wrl, scalar=sl_, in1=tn[:, 1:2], op0=ALU.mult, op1=ALU.add),
                     reads=[('wre', b), sk, 'tn1'], writes=[('xe', 1, s)])
                c.op('dve', lambda e: e.tensor_tensor(out=M[0][:], in0=wim[b][:], in1=sn_, op=ALU.mult), reads=[('wim', b), sk], writes=[MK[0]])
                c.op('dve', lambda e: e.tensor_tensor(out=M[1][:], in0=wre[b][:], in1=cs_, op=ALU.mult), reads=[('wre', b), ck], writes=[MK[1]])
                c.op('dve', lambda e: e.tensor_tensor(out=xrb[b][:], in0=M[1][:], in1=M[0][:], op=ALU.subtract), reads=[MK[0], MK[1]], writes=[('xrb', b)])
                c.op('dve', lambda e: e.tensor_tensor(out=M[2][:], in0=wre[b][:], in1=sn_, op=ALU.mult), reads=[('wre', b), sk], writes=[MK[2]])
                c.op('dve', lambda e: e.tensor_tensor(out=M[3][:], in0=wim[b][:], in1=cs_, op=ALU.mult), reads=[('wim', b), ck], writes=[MK[3]])
                c.op('dve', lambda e: e.scalar_tensor_tensor(out=xib[b][:], in0=M[2][:], scalar=-1.0, in1=M[3][:], op0=ALU.mult, op1=ALU.subtract),
                     reads=[MK[2], MK[3]], writes=[('xib', b)])

            def stageC(ic, s):
                kc, b = s // 4, s % 2
                c.op('pe', lambda e: e.matmul(PS[4 + kc][:, 0:TS], cT[0][:, s, :], xrb[b][:], start=(s % 4 == 0), stop=False),
                     reads=['cT', ('xrb', b)], writes=[('ps', 4 + kc)], pe_acc=True)
                c.op('pe', lambda e: e.matmul(PS[4 + kc][:, 0:TS], cT[1][:, s, :], xib[b][:], start=False, stop=(s % 4 == 3)),
                     reads=['cT', ('xib', b)], writes=[('ps', 4 + kc)], pe_acc=True)

            seq = [(ic, s) for ic in range(nchunk) for s in range(16)]
            stageA(*seq[0])
            stageBpre(*seq[0])
            for idx, (ic, s) in enumerate(seq):
                u, uk = ub[ic % 2], ('ub', ic % 2)
                if s == 0 and ic + 1 < nchunk:
                    c.dma('sp', ub[(ic + 1) % 2][:], uv[:, :, (ic + 1) * TS:(ic + 2) * TS], writes=[('ub', (ic + 1) % 2)])
                if idx + 1 < len(seq):
                    stageA(*seq[idx + 1])
                    stageBpre(*seq[idx + 1])
                stageB(ic, s)
                stageC(ic, s)
                if s != 15:
                    continue
                for kc in range(4):
                    c.op('dve', lambda e, kc=kc: e.scalar_tensor_tensor(out=yv[:, kc, :], in0=u[:, kc, :], scalar=dsk[:, 0, kc:kc + 1], in1=PS[4 + kc][:, 0:TS],
                                                                         op0=ALU.mult, op1=ALU.add), reads=[uk, 'dsk', ('ps', 4 + kc)], writes=['yv'])
                c.op('pool', lambda e: e.tensor_tensor(out=yt[:], in0=yv[:], in1=yv[:], op=ALU.mult), reads=['yv'], writes=['yt'])
                c.op('pool', lambda e: e.tensor_scalar(out=yt[:], in0=yt[:], scalar1=0.044715, scalar2=1.0, op0=ALU.mult, op1=ALU.add), reads=['yt'], writes=['yt'])
                c.op('pool', lambda e: e.tensor_tensor(out=yt[:], in0=yt[:], in1=yv[:], op=ALU.mult), reads=['yt', 'yv'], writes=['yt'])
                c.op('act', lambda e: e.activation(out=yt[:], in_=yt[:], func=AF.Sigmoid, scale=1.5957691216), reads=['yt'], writes=['yt'])
                c.op('dve', lambda e: e.tensor_tensor(out=yv[:], in0=yv[:], in1=yt[:], op=ALU.mult), reads=['yt', 'yv'], writes=['yv'])
                c.op('act', lambda e: e.activation(out=ygb[:], in_=yv[:], func=AF.Copy), reads=['yv'], writes=['ygb'])
                o, ok = osb[ic % 2], ('osb', ic % 2)
                for mo in range(4):
                    pg = mo % 4
                    for kc in range(4):
                        c.op('pe', lambda e, kc=kc, mo=mo: e.matmul(PS[pg][:, 0:TS], wg[:, kc, mo * 128:(mo + 1) * 128], ygb[:, kc, :], start=(kc == 0), stop=(kc == 3)),
                             reads=["wglusb", "ygb"], writes=[('ps', pg)], pe_acc=True)
                    c.op('act', lambda e, mo=mo: e.activation(out=sg[:], in_=PS[pg][:, 0:TS], func=AF.Sigmoid, bias=dsk[:, 1, mo:mo + 1]),
                         reads=[('ps', pg), 'dsk'], writes=['sg'])
                    c.op('dve', lambda e, mo=mo: e.tensor_tensor(out=o[:, mo, :], in0=yv[:, mo, :], in1=sg[:], op=ALU.mult), reads=['yv', 'sg'], writes=[ok])
                c.dma('sp', ov[:, :, ic * TS:(ic + 1) * TS], o[:], reads=[ok])
        c.barrier()
        if stop_after <= 3:
            c.es.close()
            return nc
        with ExitStack() as es:
            NCP = NCC * 128
            kslT = sb(es, "kslT", [128, L], BF16)
            kwT = sb(es, "kwT", [128, L], BF16)
            vslm = sb(es, "vslm", [128, NB, 2, 65], BF16)
            vwm = sb(es, "vwm", [128, NB, 2, 65], BF16)
            kccT = sb(es, "kccT", [128, NCP], BF16)
            vcc = sb(es, "vcc", [128, NCC, 2, 65], BF16)
            wm = sb(es, "wm", [128, 6, 512], BF16)
            visb = sb(es, "visb", [128, 8, 128], BF16)
            vfb = sb(es, "vfb", [128, 2, 256], F32)
            ovb = sb(es, "ovb", [128, NCC, 128], BF16)
            indb = sb(es, "indb", [128, L], BF16)
            c.dma('sp', kslT[:], ksl_s[:, :], writes=['kslT'])
            c.dma('sp', kwT[:], kw_s[:, :], writes=['kwT'])
            c.op('dve', lambda e: e.memset(vslm[:], 1.0), writes=['vslm'])
            c.op('dve', lambda e: e.memset(vwm[:], 1.0), writes=['vwm'])
            c.op('dve', lambda e: e.memset(vcc[:], 1.0), writes=['vcc'])
            for g in range(2):
                for c0 in range(0, NB, 8):
                    c1 = min(NB, c0 + 8)
                    c.dma('sp', vslm[:, c0:c1, g, 0:64], vsl_t[c0 * 128:c1 * 128, g * 64:(g + 1) * 64].rearrange("(c p) d -> p c d", p=128), reads=['vslm'], writes=['vslm'])
                    c.dma('sp', vwm[:, c0:c1, g, 0:64], vw_t[c0 * 128:c1 * 128, g * 64:(g + 1) * 64].rearrange("(c p) d -> p c d", p=128), reads=['vwm'], writes=['vwm'])
            c.dma('pool', wm[:], wmask[:, :, :], writes=['wm'])
            c.dma('pool', visb[:], vis8[:, :, :], writes=['visb'])
            c.dma('sp', vfb[:], vfrel[:, :, :], writes=['vfb'])
            c.dma('pool', ovb[:], ovl.rearrange("c p s -> p c s"), writes=['ovb'])
            c.dma('pool', indb[:], indneg[:, :], writes=['indb'])
            with ExitStack() as e2:
                kvT = [sb(e2, "kvT%d" % i, [128, L], BF16) for i in range(2)]
                w1c = [sb(e2, "w1c%d" % i, [128, 32, 256], BF16) for i in range(2)]
                w2p = sb(e2, "w2p", [128, 2, 2, 2, 128], BF16)
                w2v = sb(e2, "w2v", [128, 2, 64], BF16)
                pes = sb(e2, "pes", [64, 2, 32], BF16)
                bia = sb(e2, "bia", [128, 2, 2], F32)
                hid = sb(e2, "hid", [128, 2, 2, 2, NCP], BF16)
                hb = sb(e2, "hb", [128, NCP], F32)
                ht = sb(e2, "ht", [128, NCP], F32)
                c.dma('sp', kvT[0][:], kc_s[:, :], writes=[('kvT', 0)])
                c.dma('sp', kvT[1][:], vc_s[:, :], writes=[('kvT', 1)])
                c.op('pool', lambda e: e.memset(hid[:], 0.0), writes=['hid'])
                for kv in range(2):
                    for half in range(2):
                        c.dma('pool', w1c[kv][half * 64:(half + 1) * 64, :, :], cw1[kv].rearrange("(j d) h -> d j h", d=64), writes=[('w1c', kv)])
                    c.dma('pool', pes[:, kv, :], peT[kv], writes=['pes'])
                    for g in range(2):
                        for hf in range(2):
                            c.dma('pool', w2p[:, kv, g, hf, :], cw2p[kv, g, hf * 128:(hf + 1) * 128, :], writes=['w2p'])
                for hf in range(2):
                    c.dma('pool', w2v[:, hf, :], cw2[1, hf * 128:(hf + 1) * 128, :], writes=['w2v'])
                for kv in range(2):
                    for hf in range(2):
                        for j in range(32):
                            c.op('pe', lambda e, j=j: e.matmul(PS[0][:, 0:1], w1c[kv][0:64, j, hf * 128:(hf + 1) * 128], pes[:, kv, j:j + 1], start=(j == 0), stop=(j == 31)),
                                 reads=[('w1c', kv), 'pes'], writes=[('ps', 0)], pe_acc=True)
                        c.op('act', lambda e: e.activation(out=bia[:, kv, hf:hf + 1], in_=PS[0][:, 0:1], func=AF.Copy), reads=[('ps', 0)], writes=['bia'])
                for kv in range(2):
                    for g in range(2):
                        for hf in range(2):
                            pi = 1 + (g * 2 + hf) % 2
                            for j in range(32):
                                c.op('pe', lambda e, j=j: e.matmul(PS[pi][:, 0:NCMP], w1c[kv][g * 64:(g + 1) * 64, j, hf * 128:(hf + 1) * 128],
                                                                   kvT[kv][g * 64:(g + 1) * 64, j:j + 16 * (NCMP - 1) + 1:16], start=(j == 0), stop=(j == 31)),
                                     reads=[('w1c', kv), ('kvT', kv)], writes=[('ps', pi)], pe_acc=True)
                            c.op('act', lambda e: e.activation(out=hb[:, 0:NCMP], in_=PS[pi][:, 0:NCMP], func=AF.Identity, bias=bia[:, kv, hf:hf + 1]),
                                 reads=[('ps', pi), 'bia'], writes=['hb'])
                            c.op('dve', lambda e: e.tensor_tensor(out=ht[:, 0:NCMP], in0=hb[:, 0:NCMP], in1=hb[:, 0:NCMP], op=ALU.mult), reads=['hb'], writes=['ht'])
                            c.op('dve', lambda e: e.tensor_scalar(out=ht[:, 0:NCMP], in0=ht[:, 0:NCMP], scalar1=0.044715, scalar2=1.0, op0=ALU.mult, op1=ALU.add), reads=['ht'], writes=['ht'])
                            c.op('dve', lambda e: e.tensor_tensor(out=ht[:, 0:NCMP], in0=ht[:, 0:NCMP], in1=hb[:, 0:NCMP], op=ALU.mult), reads=['ht', 'hb'], writes=['ht'])
                            c.op('act', lambda e: e.activation(out=ht[:, 0:NCMP], in_=ht[:, 0:NCMP], func=AF.Sigmoid, scale=1.5957691216), reads=['ht'], writes=['ht'])
                            c.op('dve', lambda e: e.tensor_tensor(out=hid[:, kv, g, hf, 0:NCMP], in0=hb[:, 0:NCMP], in1=ht[:, 0:NCMP], op=ALU.mult), reads=['ht', 'hb', 'hid'], writes=['hid'])
                n = 0
                for g in range(2):
                    for hf in range(2):
                        c.op('pe', lambda e, n=n: e.matmul(PS[3][:, 0:NCP], w2p[:, 0, g, hf, :], hid[:, 0, g, hf, :], start=(n == 0), stop=(n == 3)),
                             reads=['w2p', 'hid'], writes=[('ps', 3)], pe_acc=True)
                        n += 1
                c.op('act', lambda e: e.activation(out=kccT[:], in_=PS[3][:, 0:NCP], func=AF.Copy), reads=[('ps', 3)], writes=['kccT'])
                for cc in range(NCC):
                    for g in range(2):
                        for hf in range(2):
                            c.op('pe', lambda e: e.matmul(PS[4][:, 0:64], hid[:, 1, g, hf, cc * 128:(cc + 1) * 128], w2v[:, hf, :], start=(hf == 0), stop=(hf == 1)),
                                 reads=['w2v', 'hid'], writes=[('ps', 4)], pe_acc=True)
                        c.op('act', lambda e: e.activation(out=vcc[:, cc, g, 0:64], in_=PS[4][:, 0:64], func=AF.Copy), reads=[('ps', 4), 'vcc'], writes=['vcc'])
                c.barrier()
            qe = [sb(es, "qe%d" % i, [128, 512], BF16) for i in range(2)]
            qo = [sb(es, "qo%d" % i, [128, 512], BF16) for i in range(2)]
            qp = [[sb(es, "qp%d_%d" % (i, g), [128, 512], BF16) for g in range(2)] for i in range(2)]
            for i in range(2):
                for g in range(2):
                    c.op('dve', lambda e, i=i, g=g: e.memset(qp[i][g][:], 0.0), writes=[('qp', i, g)])
            ge = [sb(es, "ge0", [128, 24, 128], F32)] * 2
            go = [sb(es, "go0", [128, 24, 128], F32)] * 2
            gt = [sb(es, "gt%d" % i, [128, 24, 128], F32) for i in range(2)]
            Eb = [sb(es, "Eb%d" % i, [128, 512], BF16) for i in range(3)]
            rows_ = [[sb(es, "row%d_%d" % (x, i), [128, 2, 512], F32) for i in range(2)] for x in range(3)]
            dq = []

            def popdq():
                if dq:
                    f = dq.pop(0)
                    if f is not None:
                        f()

            def drain():
                while dq:
                    popdq()
            Bc = sb(es, "Bc", [128, 512], F32)
            Bc2 = sb(es, "Bc2", [64, 512], F32)
            tmpi = sb(es, "tmpi", [128, 512], F32)
            impT = sb(es, "impT", [128, 128], F32)
            sc = sb(es, "sc", [128, 128], F32)
            sc2 = sb(es, "sc2", [128, 128], F32)
            m8 = sb(es, "m8", [128, 16], F32)
            nsl = sb(es, "nsl", [128, 128], F32)
            nsT = sb(es, "nsT", [128, 4, 128], BF16)
            acc = sb(es, "acc", [64, 512], F32)
            acct = sb(es, "acct", [64, 512], F32)
            accb = [sb(es, "accb%d" % i, [64, 512], BF16) for i in range(2)]
            ecnt = [0]
            scnt = [0]
            onv = onsa_s.rearrange("(h d) n -> d h n", d=64)

            pend = [None]

            def flush():
                if pend[0] is not None:
                    f = pend[0]
                    pend[0] = None
                    f()

            def score_chunk(KT, g, kc0, Qg, qk, extra, Vm, vkey, g_, po, first, last):
                si = scnt[0] % 2
                scnt[0] += 1
                ne = len(extra)
                c.op('pe', lambda e: e.matmul(PS[si][:, :], KT[:, kc0 * 128:(kc0 + 1) * 128], Qg, start=True, stop=(ne == 0)),
                     reads=[qk, 'kslT', 'kwT', 'kccT'], writes=[('ps', si)], pe_acc=True)
                for ix, (lt, rh, rk) in enumerate(extra):
                    c.op('pe', lambda e, lt=lt, rh=rh, ix=ix: e.matmul(PS[si][:, :], lt, rh, start=False, stop=(ix == ne - 1)),
                         reads=rk, writes=[('ps', si)], pe_acc=True)
                ei = ecnt[0] % 3
                ecnt[0] += 1
                E, ek = Eb[ei], ('E', ei)
                c.op('act', lambda e: e.activation(out=E[:], in_=PS[si][:, :], func=AF.Exp, scale=0.125), reads=[('ps', si)], writes=[ek])
                flush()
                popdq()
                return E, ek

            def fin1(po, x, g, gk_i, rw, rk):
                c.op('dve', lambda e: e.tensor_scalar(out=rw[64:65, 0, :], in0=PS[po][64:65, :], scalar1=1e-30, scalar2=None, op0=ALU.max),
                     reads=[('ps', po)], writes=[rk + '0'])
                c.op('dve', lambda e: e.reciprocal(out=rw[64:65, 0, :], in_=rw[64:65, 0, :]), reads=[rk + '0'], writes=[rk + '0'])
                gsl = gt[gk_i][64:65, g * 12 + x:g * 12 + x + 10:3, :]
                c.op('dve', lambda e: e.tensor_tensor(out=rw[64:65, 1, :].rearrange("p (h q) -> p h q", h=4), in0=rw[64:65, 0, :].rearrange("p (h q) -> p h q", h=4),
                                                      in1=gsl, op=ALU.mult), reads=[rk + '0', ('gt', gk_i)], writes=[rk + '1'])

            def fin2(po, first, rw, rk):
                c.op('pe', lambda e: e.matmul(PS[6][0:64, :], ones_f[64:65, 0:64], rw[64:65, 1, :], start=True, stop=True), reads=[rk + '1', 'ones_f'], writes=[('ps', 6)])
                c.op('act', lambda e: e.activation(out=Bc2[:], in_=PS[6][0:64, :], func=AF.Copy), reads=[('ps', 6)], writes=['Bc2'])
                if first:
                    c.op('dve', lambda e: e.tensor_tensor(out=acc[:], in0=PS[po][0:64, :], in1=Bc2[:], op=ALU.mult), reads=[('ps', po), 'Bc2'], writes=['acc'])
                else:
                    c.op('dve', lambda e: e.tensor_tensor(out=acct[:], in0=PS[po][0:64, :], in1=Bc2[:], op=ALU.mult), reads=[('ps', po), 'Bc2'], writes=['acct'])
                    c.op('pool', lambda e: e.tensor_tensor(out=acc[:], in0=acc[:], in1=acct[:], op=ALU.add), reads=['acc', 'acct'], writes=['acc'])

            for k in range(NBo):
                bi = k % 2
                qk, gk = ('qb', bi), ('gt', bi)
                c.dma('sp', qe[bi][:].rearrange("p (h q) -> p h q", h=4), q_s[:, 2 * k, :, :], writes=[('qe', bi)])
                c.dma('sp', qo[bi][:].rearrange("p (h q) -> p h q", h=4), q_s[:, 2 * k + 1, :, :], writes=[('qo', bi)])
                for g in range(2):
                    blend('dve', qp[bi][g][g * 64:(g + 1) * 64, :], qe[bi][g * 64:(g + 1) * 64, :], qo[bi][g * 64:(g + 1) * 64, :],
                          [('qe', bi), ('qo', bi)], [('qp', bi, g)], g * 64, (g + 1) * 64)
                c.dma('sp', ge[bi][64:65, :, :], g_s[:, 2 * k * 128:(2 * k + 1) * 128].rearrange("(o r) q -> o r q", o=1), writes=[('ge', 0)])
                c.dma('sp', go[bi][64:65, :, :], g_s[:, (2 * k + 1) * 128:(2 * k + 2) * 128].rearrange("(o r) q -> o r q", o=1), writes=[('go', 0)])
                blend('pool', gt[bi][64:65, :, :], ge[bi][64:65, :, :], go[bi][64:65, :, :], [('ge', 0), ('go', 0)], [gk], 64, 65)
                ccd = k // 8
                for g in range(2):
                    Qg = qp[bi][g][:]
                    qk = ('qp', bi, g)
                    for cc in range(ccd + 1):
                        E, ek = score_chunk(kccT, g, cc, Qg, qk, [], None, None, None, 2, cc == 0, cc == ccd)
                        if cc == ccd:
                            for h in range(4):
                                c.op('pool' if h % 2 else 'dve', lambda e, h=h: e.tensor_tensor(out=E[:, h * 128:(h + 1) * 128], in0=E[:, h * 128:(h + 1) * 128],
                                                                                                 in1=visb[:, k % 8, :], op=ALU.mult), reads=[ek, 'visb'], writes=[ek])
                        def pv_cmp(cc=cc, E=E, ek=ek, g=g, ccd=ccd):
                            c.op('pe', lambda e: e.matmul(PS[2][0:65, :], vcc[:, cc, g, :], E[:], start=(cc == 0), stop=(cc == ccd)),
                                 reads=[ek, 'vcc'], writes=[('ps', 2)], pe_acc=True)
                            c.op('pe', lambda e: e.matmul(PS[5][:, :], ovb[:, cc, :], E[:], start=(cc == 0), stop=(cc == ccd)),
                                 reads=[ek, 'ovb'], writes=[('ps', 5)], pe_acc=True)
                        pend[0] = pv_cmp
                    flush()
                    par_ = (k * 2 + g) % 2
                    rc_, rw_, rs_ = rows_[0][par_], rows_[1][par_], rows_[2][par_]
                    kc_, kw_, ks_ = 'rowc%d' % par_, 'roww%d' % par_, 'rows%d' % par_
                    fin1(2, 0, g, bi, rc_, kc_)

                    def stepB(rc_=rc_, kc_=kc_):
                        c.op('pe', lambda e: e.matmul(PS[6][:, :], ones_f[64:65, :], rc_[64:65, 0, :], start=True, stop=True), reads=[kc_ + '0', 'ones_f'], writes=[('ps', 6)])
                        c.op('act', lambda e: e.activation(out=Bc[:], in_=PS[6][:, :], func=AF.Copy), reads=[('ps', 6)], writes=['Bc'])
                        c.op('dve', lambda e: e.tensor_tensor(out=tmpi[:], in0=PS[5][:, :], in1=Bc[:], op=ALU.mult), reads=[('ps', 5), 'Bc'], writes=['tmpi'])
                        c.op('dve', lambda e: e.tensor_reduce(out=impT[:], in_=tmpi[:].rearrange("p (h q) -> p q h", h=4), axis=AX.X, op=ALU.add), reads=['tmpi'], writes=['impT'])

                    def stepS1(k=k):
                        c.op('pe', lambda e: e.transpose(PS[7][:, 0:128], impT[:], id_f[:]), reads=['impT', 'id_f'], writes=[('ps', 7)])
                        vo = 128 - 4 * k
                        c.op('dve', lambda e: e.tensor_tensor(out=sc[:], in0=PS[7][:, 0:128], in1=vfb[:, 0, vo:vo + 128], op=ALU.mult), reads=[('ps', 7), 'vfb'], writes=['sc'])
                        c.op('dve', lambda e: e.tensor_tensor(out=sc[:], in0=sc[:], in1=vfb[:, 1, vo:vo + 128], op=ALU.add), reads=['sc', 'vfb'], writes=['sc'])
                        c.op('dve', lambda e: e.memset(sc[:, 0:1], BIG), reads=['sc'], writes=['sc'])
                        c.op('dve', lambda e: e.max(out=m8[:, 0:8], in_=sc[:]), reads=['sc'], writes=['m8'])
                        c.op('dve', lambda e: e.match_replace(out=sc2[:], in_to_replace=m8[:, 0:8], in_values=sc[:], imm_value=-3.0e38), reads=['sc', 'm8'], writes=['sc2'])
                        c.op('dve', lambda e: e.max(out=m8[:, 8:16], in_=sc2[:]), reads=['sc2', 'm8'], writes=['m8'])
                        c.op('dve', lambda e: e.tensor_scalar(out=nsl[:], in0=sc[:], scalar1=m8[:, 15:16], scalar2=None, op0=ALU.is_lt), reads=['sc', 'm8'], writes=['nsl'])

                    def stepS2():
                        c.op('pe', lambda e: e.transpose(PS[7][:, 128:256], nsl[:], id_f[:]), reads=['nsl', 'id_f'], writes=[('ps', 7)])
                        for h in range(4):
                            c.op('act' if h % 2 else 'dve', lambda e, h=h: (e.activation(out=nsT[:, h, :], in_=PS[7][:, 128:256], func=AF.Copy) if h % 2
                                                                            else e.tensor_copy(out=nsT[:, h, :], in_=PS[7][:, 128:256])),
                                 reads=[('ps', 7), 'nsT'], writes=['nsT'])

                    dq.extend([stepB, lambda rc_=rc_, kc_=kc_: fin2(2, True, rc_, kc_), stepS1, None, None, stepS2])
                    wl = [wi for wi in range(6) if 2 * k - 4 + wi >= 0]
                    for wi in wl:
                        kcn = 2 * k - 4 + wi
                        E, ek = score_chunk(kwT, g, kcn, Qg, qk, [(id_bf[:], wm[:, wi, :], ['id_bf', 'wm'])], None, None, None, 3, False, False)

                        def pv_win(kcn=kcn, wi=wi, E=E, ek=ek, g=g, wl=wl):
                            c.op('pe', lambda e: e.matmul(PS[3][0:65, :], vwm[:, kcn, g, :], E[:], start=(wi == wl[0]), stop=(wi == wl[-1])),
                                 reads=[ek, 'vwm'], writes=[('ps', 3)], pe_acc=True)
                        pend[0] = pv_win
                    flush()
                    fin1(3, 2, g, bi, rw_, kw_)
                    drain()
                    dq.append(lambda rw_=rw_, kw_=kw_: fin2(3, False, rw_, kw_))
                    nk = 2 * k + 2
                    for kcn in range(nk):
                        extra = [(indb[:, kcn * 128:(kcn + 1) * 128], nsT[:].rearrange("p h q -> p (h q)"), ['indb', 'nsT'])]
                        if kcn >= 2 * k:
                            extra.append((id_bf[:], wm[:, 4 + kcn - 2 * k, :], ['id_bf', 'wm']))
                        E, ek = score_chunk(kslT, g, kcn, Qg, qk, extra, None, None, None, 4, False, False)

                        def pv_slc(kcn=kcn, E=E, ek=ek, g=g, nk=nk):
                            c.op('pe', lambda e: e.matmul(PS[4][0:65, :], vslm[:, kcn, g, :], E[:], start=(kcn == 0), stop=(kcn == nk - 1)),
                                 reads=[ek, 'vslm'], writes=[('ps', 4)], pe_acc=True)
                        pend[0] = pv_slc
                    flush()
                    drain()
                    fin1(4, 1, g, bi, rs_, ks_)

                    def stepE(rs_=rs_, ks_=ks_, g=g, k=k):
                        fin2(4, False, rs_, ks_)
                        ab, abk = accb[g], ('accb', g)
                        c.op('act', lambda e: e.activation(out=ab[:], in_=acc[:], func=AF.Copy), reads=['acc'], writes=[abk])
                        c.dma('sp', onv[:, g * 4:(g + 1) * 4, k * 128:(k + 1) * 128], ab[:].rearrange("p (h q) -> p h q", h=4), reads=[abk])
                    dq.append(stepE)
            drain()
        c.barrier()

        if stop_after <= 4:
            c.es.close()
            return nc
        with ExitStack() as es:
            T = 512
            wo = load_w(es, "wo", wout, 8, D)
            hx = sb(es, "hx", [128, 8, T], F32)
            he = sb(es, "he", [128, 8, T], F32)
            ho = sb(es, "ho", [128, 8, T], F32)
            oo = sb(es, "oo", [128, 8, T], BF16)
            oe_ = sb(es, "oe_", [128, 4, T], BF16)
            od_ = sb(es, "od_", [128, 4, T], BF16)
            h1b = ch(h1_s).rearrange("p c (b j q) -> p c b j q", j=2, q=128)
            o5b = ch(os5_s).rearrange("p c (b j q) -> p c b j q", j=2, q=128)
            onb = ch(onsa_s)
            h2v = ch(h2_s)
            v4 = lambda t: t[:].rearrange("p c (b q) -> p c b q", q=128)
            for it in range(Lo // T):
                for kc in range(8):
                    c.dma('sp', v4(he)[:, kc], h1b[:, kc, 4 * it:4 * it + 4, 0, :], writes=['he'])
                    c.dma('sp', v4(ho)[:, kc], h1b[:, kc, 4 * it:4 * it + 4, 1, :], writes=['ho'])
                for kc in range(4):
                    c.dma('sp', v4(oe_)[:, kc], o5b[:, kc, 4 * it:4 * it + 4, 0, :], writes=['oe_'])
                    c.dma('sp', v4(od_)[:, kc], o5b[:, kc, 4 * it:4 * it + 4, 1, :], writes=['od_'])
                c.dma('sp', oo[:, 0:4, :], onb[:, :, it * T:(it + 1) * T], writes=['oo_n'])
                blend('dve', hx[:], he[:], ho[:], ['he', 'ho'], ['hx'])
                blend('pool', oo[:, 4:8, :], oe_[:], od_[:], ['oe_', 'od_'], ['oo_s'])
                for mo in range(8):
                    po = mo % 2
                    for kc in range(8):
                        c.op('pe', lambda e, kc=kc, mo=mo: e.matmul(PS[po][:, :], wo[:, kc, mo * 128:(mo + 1) * 128], oo[:, kc, :], start=(kc == 0), stop=(kc == 7)),
                             reads=['wo', 'oo_n', 'oo_s'], writes=[('ps', po)], pe_acc=True)
                    c.op('dve', lambda e, mo=mo: e.tensor_tensor(out=hx[:, mo, :], in0=PS[po][:, :], in1=hx[:, mo, :], op=ALU.add), reads=[('ps', po), 'hx'], writes=['hx'])
                c.dma('sp', h2v[:, :, it * T:(it + 1) * T], hx[:], reads=['hx'])
        c.barrier()

        ffn_phase(h2_s, h3_s, Lo, f2w1, f2w3, f2w2, 2, "B")

        with ExitStack() as es:
            T = 512
            wgt = load_w(es, "wgt", wgate, 8, D)
            wpl = load_w(es, "wpl", wple, 2, D)
            xts = [sb(es, "cxt%d" % i, [128, 8, T], F32) for i in range(2)]
            sq = sb(es, "csq", [128, 8, T], BF16)
            a = sb(es, "ca", [128, 8, T], BF16)
            rstd = sb(es, "crs", [128, T], F32)
            pfs = [sb(es, "cpf%d" % i, [128, 2, T], F32) for i in range(2)]
            pb = sb(es, "cpb", [128, 2, T], BF16)
            gt_ = sb(es, "cgt", [128, T], F32)
            tt = sb(es, "ctt", [128, T], F32)
            ofs = [sb(es, "cof%d" % i, [128, 8, T], F32) for i in range(2)]
            h3v, pv, ov_ = ch(h3_s), ch(pT), ch(outT)
            ntl = Lo // T

            def ld(it):
                c.dma('sp', xts[it % 2][:], h3v[:, :, it * T:(it + 1) * T], writes=[('cxt', it % 2)])
                c.dma('sp', pfs[it % 2][:], pv[:, :, it * T:(it + 1) * T], writes=[('cpf', it % 2)])
            ld(0)
            for it in range(ntl):
                xt, xk = xts[it % 2], ('cxt', it % 2)
                pf, pk = pfs[it % 2], ('cpf', it % 2)
                of, ofk = ofs[it % 2], ('cof', it % 2)
                if it + 1 < ntl:
                    ld(it + 1)
                c.op('act', lambda e: e.activation(out=pb[:], in_=pf[:], func=AF.Copy), reads=[pk], writes=['pb'])
                rmsnorm((sq, rstd), xt[:], xk, 3, T, a, 'a', 6)
                for mo in range(8):
                    pg, pp = mo % 2, 2 + mo % 2
                    for kc in range(8):
                        c.op('pe', lambda e, kc=kc, mo=mo: e.matmul(PS[pg][:, :], wgt[:, kc, mo * 128:(mo + 1) * 128], a[:, kc, :], start=(kc == 0), stop=(kc == 7)),
                             reads=['wgt', 'a'], writes=[('ps', pg)], pe_acc=True)
                    for kc in range(2):
                        c.op('pe', lambda e, kc=kc, mo=mo: e.matmul(PS[pp][:, :], wpl[:, kc, mo * 128:(mo + 1) * 128], pb[:, kc, :], start=(kc == 0), stop=(kc == 1)),
                             reads=['wpl', 'pb'], writes=[('ps', pp)], pe_acc=True)
                    c.op('act', lambda e: e.activation(out=gt_[:], in_=PS[pg][:, :], func=AF.Sigmoid), reads=[('ps', pg)], writes=['gt_'])
                    c.op('dve', lambda e: e.tensor_tensor(out=tt[:], in0=PS[pp][:, :], in1=gt_[:], op=ALU.mult), reads=[('ps', pp), 'gt_'], writes=['tt'])
                    c.op('pool', lambda e, mo=mo: e.tensor_tensor(out=xt[:, mo, :], in0=xt[:, mo, :], in1=tt[:], op=ALU.add), reads=['tt', xk], writes=[xk])
                rmsnorm((sq, rstd), xt[:], xk, 4, T, of, ofk, 6)
                c.dma('act', ov_[:, :, it * T:(it + 1) * T], of[:], reads=[ofk])
        c.barrier()
    c.es.close()
    return nc


def _host_consts(L, j):
    NCMP = L // 16 - 1
    NCC = max(1, L // 2048)
    pI = np.arange(128)[:, None]
    iI = np.arange(128)[None, :]
    tri = np.where(pI > iI, NEGM, 0.0).astype(np.float32)
    part = np.where(pI <= iI, NEGM, 0.0).astype(np.float32)
    full = np.zeros((128, 128), np.float32)
    none = np.full((128, 128), NEGM, np.float32)
    lst = [part, full, full, full, tri, none] if j == 0 else [none, part, full, full, full, tri]
    wmask = np.stack([np.tile(m, (1, 4)) for m in lst], axis=1).astype(np.float32)
    kk = np.arange(8)[None, :, None]
    vis8 = ((16 * pI[:, :, None] + 31) <= (256 * kk + 128 * j + iI[:, None, :])).astype(np.float32)
    q = np.arange(128)[:, None]
    sp = np.arange(256)[None, :] - 128
    cur = 2 * j + (q >= 64)
    valid = sp <= cur
    forced = (sp == cur) | (sp == cur - 1)
    V = (valid & ~forced).astype(np.float32)
    Fm = np.where(valid & forced, BIG, np.where(valid, 0.0, -BIG)).astype(np.float32)
    vfrel = np.stack([V, Fm], axis=1).astype(np.float32)
    m = (np.arange(NCC * 128)).reshape(NCC, 128, 1)
    s = np.arange(128)[None, None, :]
    ovl = ((16 * m < 64 * s + 64) & (16 * m + 32 > 64 * s) & (m < NCMP)).astype(np.float32)
    key = np.arange(L)[None, :]
    indneg = np.where(key // 64 == np.arange(128)[:, None], NEGM, 0.0).astype(np.float32)
    par = np.zeros((128, 2), np.float32)
    par[:, 0] = j
    par[:, 1] = 1 - j
    return dict(wmask=wmask, vis8=vis8, vfrel=vfrel, ovl=ovl, indneg=indneg, par=par)


def _host_weights(I, L):
    f = lambda a: np.ascontiguousarray(np.asarray(a, dtype=np.float32))
    gl = lambda v: f(np.asarray(v).reshape(8, 128).T)
    gains = np.stack([gl(I["norm_ffn1"][0]), gl(I["norm_mix"][0]), gl(I["norm_ffn2"][0]), gl(I["norm_ple"][0]), gl(I["norm_final"])])
    w_in = np.asarray(I["w_in"][0])
    cols = []
    for h in range(4):
        for g in range(2):
            cols += list(range((g * 4 + h) * 64, (g * 4 + h) * 64 + 64))
    cols += list(range(512, 640)) + list(range(768, 896)) + list(range(1024, 1152))
    cols += list(range(640, 768)) + list(range(896, 1024)) + list(range(1152, 1280))
    cols += list(range(1304, 1816)) + list(range(1280, 1304))
    cols = np.array(cols)
    winp = f(w_in[:, cols])
    c0 = cols[:896]
    sw = (c0 // 64) * 64 + (c0 % 64 + 32) % 64
    wins = f(w_in[:, sw])
    p = np.arange(128)
    d = p % 64
    inv = (np.float32(10000.0) ** (-(d % 32).astype(np.float32) / np.float32(32))).astype(np.float32)
    pos = np.arange(L, dtype=np.float32)
    ang = (pos[None, :] * inv[:, None]).astype(np.float32)
    ropec = np.cos(ang).astype(np.float32)
    ropes = (np.sin(ang) * np.where(d < 32, -1.0, 1.0)[:, None]).astype(np.float32)
    peT = np.stack([f(np.asarray(I["cmp_pe_k"][0]).T), f(np.asarray(I["cmp_pe_v"][0]).T)])
    cw1 = np.stack([f(I["cmp_wk1"][0]), f(I["cmp_wv1"][0])])
    cw2 = np.stack([f(I["cmp_wk2"][0]), f(I["cmp_wv2"][0])])
    cw2p = np.zeros((2, 2, 256, 128), np.float32)
    for kv in range(2):
        for g in range(2):
            cw2p[kv, g, :, g * 64:(g + 1) * 64] = cw2[kv]
    are, aim, ldt = np.asarray(I["s5_a_re"][0]), np.asarray(I["s5_a_im"][0]), np.asarray(I["s5_log_dt"][0])
    s5p = np.zeros((3, 128, 16), np.float32)
    s5B = np.zeros((2, 16, 128, 128), np.float32)
    s5C = np.zeros((2, 16, 128, 128), np.float32)
    Bre, Bim = np.asarray(I["s5_b_re"][0]), np.asarray(I["s5_b_im"][0])
    Cre, Cim = np.asarray(I["s5_c_re"][0]), np.asarray(I["s5_c_im"][0])
    for s in range(16):
        for hh in range(2):
            g = 2 * s + hh
            s5p[0, hh * 64:(hh + 1) * 64, s] = are[g]
            s5p[1, hh * 64:(hh + 1) * 64, s] = aim[g]
            s5p[2, hh * 64:(hh + 1) * 64, s] = ldt[g]
            c0_ = (g % 8) * 16
            s5B[0, s, hh * 64:(hh + 1) * 64, c0_:c0_ + 16] = Bre[g]
            s5B[1, s, hh * 64:(hh + 1) * 64, c0_:c0_ + 16] = Bim[g]
            s5C[0, s, hh * 64:(hh + 1) * 64, c0_:c0_ + 16] = Cre[g].T
            s5C[1, s, hh * 64:(hh + 1) * 64, c0_:c0_ + 16] = Cim[g].T
    s5d = np.stack([f(np.asarray(I["s5_d"][0]).reshape(4, 128).T), f(np.asarray(I["s5_b_glu"][0]).reshape(4, 128).T)])
    return dict(gains=f(gains), f1w1=f(I["ffn1_w1"][0]), f1w3=f(I["ffn1_w3"][0]), f1w2=f(I["ffn1_w2"][0]),
                f2w1=f(I["ffn2_w1"][0]), f2w3=f(I["ffn2_w3"][0]), f2w2=f(I["ffn2_w2"][0]),
                winp=winp, wins=wins, ropec=ropec, ropes=ropes, peT=f(peT), cw1=f(cw1), cw2=f(cw2), cw2p=cw2p,
                s5p=s5p, s5B=s5B, s5C=s5C, s5d=f(s5d), wglu=f(I["s5_w_glu"][0]),
                wout=f(I["w_out"][0]), wgate=f(I["w_ple_gate"][0]), wple=f(I["w_ple"][0]))


def make_in_maps(I):
    x = np.asarray(I["x"]); p = np.asarray(I["p"])
    B, L = x.shape[0], x.shape[1]
    W = _host_weights(I, L)
    maps = []
    for b in range(B):
        xT = np.ascontiguousarray(x[b].T)
        pb = p[0, b].reshape(L // 256, 2, 128, 256)
        for j in range(2):
            m = dict(W)
            m.update(_host_consts(L, j))
            m["xT"] = xT
            m["pT"] = np.ascontiguousarray(pb[:, j].reshape(L // 2, 256).T)
            maps.append(m)
    return maps, B, L


def assemble(results, B, L):
    out = np.zeros((B, L, D), np.float32)
    ov = out.reshape(B, L // 256, 2, 128, D)
    for b in range(B):
        for j in range(2):
            o = np.asarray(results[b * 2 + j]["outT"], dtype=np.float32)
            ov[b, :, j] = o.T.reshape(L // 256, 128, D)
    return out


def kernel(**inputs):
    maps, B, L = make_in_maps(inputs)
    nc = build(L)
    res = run_bass_kernel_spmd(nc, maps, core_ids=list(range(len(maps))))
    return assemble(res.results, B, L)
```
